# Optimizing a Trainium2 kernel written in Bass

```python
import math
import jax, jax.numpy as jnp
from jax import lax
import numpy as np

D_MODEL = 1024
BATCH = 32
SEQ = 256
DEPTH = 1
DEC_BATCH = 2
DEC_SEQ = 1024
PAST_LEN = 512

GRID_W = 64
MIX_WIDTH = D_MODEL
ATTN_WIDTH = MIX_WIDTH // 2
CONV_WIDTH = MIX_WIDTH - ATTN_WIDTH
N_HEADS = 4
HEAD_DIM = ATTN_WIDTH // (2 * N_HEADS)
V_DIM = 2 * HEAD_DIM
CONV_K = 3
D_FF = 2816
ROPE_BASE = 10000.0
EPS = 1e-6
Q_BLOCK = 128
N_SUB = 3
IN_WIDTH = 3 * ATTN_WIDTH + 3 * CONV_WIDTH

kernel_name = "hybrid_diffattn_shortconv_prefix_dit_step"


def rmsnorm(x, g):
    xf = x.astype(jnp.float32)
    y = xf * lax.rsqrt(jnp.mean(xf * xf, axis=-1, keepdims=True) + EPS)
    return (y * g.astype(jnp.float32)).astype(x.dtype)


def modulation(cond, w_mod, b_mod):
    m = jax.nn.silu(cond) @ w_mod + b_mod
    return m.reshape(cond.shape[0], N_SUB, 3, D_MODEL)


def swiglu(u, w_up, w_down):
    a, b = jnp.split(u @ w_up, 2, axis=-1)
    return (jax.nn.silu(a) * b) @ w_down


def short_conv(u, w):
    T = u.shape[1]
    up = jnp.pad(u, ((0, 0), (1, 1), (0, 0)))
    return up[:, :T] * w[0] + up[:, 1:T + 1] * w[1] + up[:, 2:] * w[2]


def _rot_half(x, cos, sin):
    x1, x2 = jnp.split(x, 2, axis=-1)
    return jnp.concatenate([x1 * cos - x2 * sin, x2 * cos + x1 * sin], axis=-1)


def axial_rope(x):
    T = x.shape[1]
    rows = T // GRID_W
    row = jnp.repeat(jnp.arange(rows, dtype=jnp.float32), GRID_W)
    col = jnp.tile(jnp.arange(GRID_W, dtype=jnp.float32), rows)
    half = HEAD_DIM // 2
    freqs = ROPE_BASE ** (-jnp.arange(0, half, 2, dtype=jnp.float32) / half)
    ang_r = (row[:, None] * freqs)[None, :, None, None, :]
    ang_c = (col[:, None] * freqs)[None, :, None, None, :]
    dt = x.dtype
    xr = _rot_half(x[..., :half], jnp.cos(ang_r).astype(dt), jnp.sin(ang_r).astype(dt))
    xc = _rot_half(x[..., half:], jnp.cos(ang_c).astype(dt), jnp.sin(ang_c).astype(dt))
    return jnp.concatenate([xr, xc], axis=-1)


def diff_attention(q, k, v, lam):
    B, Tq = q.shape[0], q.shape[1]
    qb = math.gcd(Q_BLOCK, Tq)
    nb = Tq // qb
    scale = HEAD_DIM ** -0.5

    def block(qblk):
        s = jnp.einsum('bqhid,bkhid->bhiqk', qblk, k).astype(jnp.float32) * scale
        p = jax.nn.softmax(s, axis=-1)
        pd = (p[:, :, 0] - lam * p[:, :, 1]).astype(v.dtype)
        return jnp.einsum('bhqk,bkhe->bqhe', pd, v)

    qs = q.reshape(B, nb, qb, N_HEADS, 2, HEAD_DIM).swapaxes(0, 1)
    o = lax.map(block, qs)
    return o.swapaxes(0, 1).reshape(B, Tq, N_HEADS, V_DIM)


def mixer(u, w_in, conv_w, lam_qk, subln_g, w_o, lambda_init, ctx_k, ctx_v):
    N, T, _ = u.shape
    A, C = ATTN_WIDTH, CONV_WIDTH
    proj = u @ w_in
    q, k, v, bg, cg, xc = jnp.split(proj, [A, 2 * A, 3 * A, 3 * A + C, 3 * A + 2 * C], axis=-1)
    q = q.reshape(N, T, N_HEADS, 2, HEAD_DIM)
    k = k.reshape(N, T, N_HEADS, 2, HEAD_DIM)
    v = v.reshape(N, T, N_HEADS, V_DIM)
    lq = lam_qk.astype(jnp.float32)
    lam = jnp.exp(jnp.sum(lq[0] * lq[1])) - jnp.exp(jnp.sum(lq[2] * lq[3])) + lambda_init
    if ctx_k is None:
        k_all, v_all = k, v
    else:
        q = axial_rope(q)
        k_lat = axial_rope(k)
        L = ctx_k.shape[1]
        k_all = jnp.concatenate([ctx_k.reshape(N, L, N_HEADS, 2, HEAD_DIM), k_lat], axis=1)
        v_all = jnp.concatenate([ctx_v, v], axis=1)
    o = diff_attention(q, k_all, v_all, lam)
    o = rmsnorm(o, subln_g) * (1.0 - lambda_init)
    conv_out = bg * short_conv(cg * xc, conv_w)
    out = jnp.concatenate([o.reshape(N, T, A), conv_out], axis=-1) @ w_o
    return out, k.reshape(N, T, N_HEADS, 2 * HEAD_DIM), v


def trunk_layer(x, mod, ln_pre, ln_post, f1u, f1d, f2u, f2d, w_in, conv_w, lam_qk, subln_g, w_o,
                lambda_init, ctx_k, ctx_v):
    def smg(i):
        return mod[:, i, 0, None, :], mod[:, i, 1, None, :], mod[:, i, 2, None, :]

    sh, sc, gt = smg(0)
    u = rmsnorm(x, ln_pre[0]) * (1.0 + sc) + sh
    x = x + 0.5 * gt * rmsnorm(swiglu(u, f1u, f1d), ln_post[0])

    sh, sc, gt = smg(1)
    u = rmsnorm(x, ln_pre[1]) * (1.0 + sc) + sh
    mix_out, k, v = mixer(u, w_in, conv_w, lam_qk, subln_g, w_o, lambda_init, ctx_k, ctx_v)
    x = x + gt * rmsnorm(mix_out, ln_post[1])

    sh, sc, gt = smg(2)
    u = rmsnorm(x, ln_pre[2]) * (1.0 + sc) + sh
    x = x + 0.5 * gt * rmsnorm(swiglu(u, f2u, f2d), ln_post[2])
    return x, k, v


def setup_inputs(seed: int = 0) -> dict:
    key = jax.random.key(seed)
    ks = jax.random.split(key, 20)
    f32 = jnp.float32

    def nrm(k, shape, s):
        return jax.random.normal(k, shape, f32) * s

    return {
        "x_prompt": nrm(ks[0], (BATCH, SEQ, D_MODEL), 1.0),
        "x_sample": nrm(ks[1], (DEC_BATCH, DEC_SEQ, D_MODEL), 1.0),
        "c": nrm(ks[2], (DEC_BATCH, D_MODEL), 1.0),
        "cache_k": nrm(ks[3], (DEC_BATCH, DEPTH, PAST_LEN, N_HEADS, 2 * HEAD_DIM), 1.0),
        "cache_v": nrm(ks[4], (DEC_BATCH, DEPTH, PAST_LEN, N_HEADS, V_DIM), 1.0),
        "c_ctx": nrm(ks[5], (D_MODEL,), 1.0),
        "w_mod": nrm(ks[6], (DEPTH, D_MODEL, N_SUB * 3 * D_MODEL), 0.5 * D_MODEL ** -0.5),
        "b_mod": nrm(ks[7], (DEPTH, N_SUB * 3 * D_MODEL), 0.02),
        "norm_pre": 1.0 + nrm(ks[8], (DEPTH, N_SUB, D_MODEL), 0.02),
        "norm_post": 1.0 + nrm(ks[9], (DEPTH, N_SUB, D_MODEL), 0.02),
        "ffn1_up": nrm(ks[10], (DEPTH, D_MODEL, 2 * D_FF), D_MODEL ** -0.5),
        "ffn1_down": nrm(ks[11], (DEPTH, D_FF, D_MODEL), D_FF ** -0.5),
        "ffn2_up": nrm(ks[12], (DEPTH, D_MODEL, 2 * D_FF), D_MODEL ** -0.5),
        "ffn2_down": nrm(ks[13], (DEPTH, D_FF, D_MODEL), D_FF ** -0.5),
        "w_in": nrm(ks[14], (DEPTH, D_MODEL, IN_WIDTH), D_MODEL ** -0.5),
        "conv_w": nrm(ks[15], (DEPTH, CONV_K, CONV_WIDTH), CONV_K ** -0.5),
        "lam_qk": nrm(ks[16], (DEPTH, 4, HEAD_DIM), 0.1),
        "subln_g": 1.0 + nrm(ks[17], (DEPTH, V_DIM), 0.02),
        "w_o": nrm(ks[18], (DEPTH, MIX_WIDTH, D_MODEL), MIX_WIDTH ** -0.5),
    }


def reference(x_prompt, x_sample, c, cache_k, cache_v, c_ctx, w_mod, b_mod, norm_pre, norm_post,
              ffn1_up, ffn1_down, ffn2_up, ffn2_down, w_in, conv_w, lam_qk, subln_g, w_o):
    h = x_prompt
    new_k, new_v = [], []
    for l in range(DEPTH):
        lambda_init = 0.8 - 0.6 * math.exp(-0.3 * l)
        mod = modulation(c_ctx[None, :], w_mod[l], b_mod[l])
        h, k_l, v_l = trunk_layer(h, mod, norm_pre[l], norm_post[l], ffn1_up[l], ffn1_down[l],
                                  ffn2_up[l], ffn2_down[l], w_in[l], conv_w[l], lam_qk[l],
                                  subln_g[l], w_o[l], lambda_init, None, None)
        new_k.append(k_l)
        new_v.append(v_l)
    y_prompt = h

    h = x_sample
    for l in range(DEPTH):
        lambda_init = 0.8 - 0.6 * math.exp(-0.3 * l)
        mod = modulation(c, w_mod[l], b_mod[l])
        h, _, _ = trunk_layer(h, mod, norm_pre[l], norm_post[l], ffn1_up[l], ffn1_down[l],
                              ffn2_up[l], ffn2_down[l], w_in[l], conv_w[l], lam_qk[l],
                              subln_g[l], w_o[l], lambda_init, cache_k[:, l], cache_v[:, l])
    y_sample = h
    return (y_prompt, y_sample, jnp.stack(new_k, axis=1), jnp.stack(new_v, axis=1))
```

```python
import math
import numpy as np
import concourse.bass as bass
import concourse.mybir as mybir
from concourse.bass_utils import run_bass_kernel_spmd
from contextlib import ExitStack

F32 = mybir.dt.float32
BF16 = mybir.dt.bfloat16
AF = mybir.ActivationFunctionType
ALU = mybir.AluOpType
AX = mybir.AxisListType

DMA_K = 6
NT = 1280
BLK = [(0, 512), (512, 1024), (1024, 1280)]
GRP = [(0, 1024), (1024, 1280)]
EPS = 1e-6
NEG = -30000.0
SPL = 1024
LAMBDA_INIT = 0.2
C_BMOD = 0
C_GPRE = 144
C_GPOST = 192
C_CONVW = 240
C_FLAG = 252
C_MASK = 253
C_LAM = 290
C_COND = 546
NCST = 562


class Buf:
    __slots__ = ("name", "lw", "rd")

    def __init__(self, name):
        self.name = name
        self.lw = None
        self.rd = {}


class Multi(list):
    pass


def _flat(bufs):
    out = []
    for b in bufs:
        if isinstance(b, Multi):
            out.extend(b)
        else:
            out.append(b)
    return out


class Op:
    __slots__ = ("eng", "fn", "deps", "needed", "tok", "is_dma", "qidx")

    def __init__(self, eng, fn, is_dma):
        self.eng = eng
        self.fn = fn
        self.deps = []
        self.needed = False
        self.tok = None
        self.is_dma = is_dma
        self.qidx = -1


class Sched:
    ENGS = ("pe", "act", "dve", "pool", "sp")

    def __init__(self):
        self.prog = {e: [] for e in self.ENGS}
        self.dmas = {e: [] for e in self.ENGS}

    def _dep(self, op, prod, raw, cross_only=False):
        if prod is None or prod is op:
            return
        if (not prod.is_dma) and prod.eng == op.eng and not op.is_dma:
            if op.eng == "pe" or cross_only:
                return
        prod.needed = True
        op.deps.append(prod)

    def op(self, eng, fn, reads=(), writes=(), dma=False, excl=()):
        o = Op(eng, fn, dma)
        reads = _flat(reads)
        writes = _flat(writes)
        for b in excl:
            self._dep(o, b.lw, False, cross_only=True)
            for r in b.rd.values():
                self._dep(o, r, False, cross_only=True)
        for b in reads:
            self._dep(o, b.lw, True)
        for b in writes:
            self._dep(o, b.lw, False)
            for r in b.rd.values():
                self._dep(o, r, False)
        if dma:
            q = self.dmas[eng]
            o.qidx = len(q)
            if o.qidx >= DMA_K:
                prev = q[o.qidx - DMA_K]
                prev.needed = True
                o.deps.append(prev)
            q.append(o)
            o.needed = True
        for b in reads:
            key = id(o) if dma else eng
            b.rd[key] = o
        for b in writes:
            b.lw = o
            b.rd = {}
        for b in excl:
            b.lw = o
            b.rd = {}
        self.prog[eng].append(o)
        return o

    def alias(self, new_bufs, old_bufs):
        new_bufs = _flat(new_bufs)
        old_bufs = _flat(old_bufs)
        users = {}
        for ob in old_bufs:
            if ob.lw is not None:
                users[id(ob.lw)] = ob.lw
            for r in ob.rd.values():
                users[id(r)] = r
        for nb in new_bufs:
            if nb.lw is not None:
                users[id(nb.lw)] = nb.lw
            for r in nb.rd.values():
                users[id(r)] = r
        for nb in new_bufs:
            nb.lw = None
            nb.rd = dict(users)

    def emit(self, nc, stack):
        sems = {}
        for e in self.ENGS:
            sems[e] = stack.enter_context(nc.semaphore("s_" + e))
        dsems = {}
        for e in self.ENGS:
            if self.dmas[e]:
                dsems[e] = [stack.enter_context(nc.semaphore("d_%s%d" % (e, i))) for i in range(DMA_K)]
        for e in self.ENGS:
            cnt = 0
            for o in self.prog[e]:
                if o.is_dma:
                    o.tok = (("d", e, o.qidx % DMA_K), dsems[e][o.qidx % DMA_K], 16 * (o.qidx // DMA_K + 1))
                elif o.needed:
                    cnt += 1
                    o.tok = (("c", e), sems[e], cnt)
        handles = {"pe": nc.tensor, "act": nc.scalar, "dve": nc.vector, "pool": nc.gpsimd, "sp": nc.sync}

        def run(e):
            h = handles[e]
            known = {}
            for o in self.prog[e]:
                need = {}
                for d in o.deps:
                    key, sem, val = d.tok
                    if known.get(key, 0) >= val:
                        continue
                    if key not in need or need[key][1] < val:
                        need[key] = (sem, val)
                for key, (sem, val) in need.items():
                    h.wait_ge(sem, val)
                    known[key] = val
                ins = o.fn(h)
                if o.tok is not None:
                    ins.then_inc(o.tok[1], 16 if o.is_dma else 1)
            q = self.dmas[e]
            if q:
                last = {}
                for o in q:
                    last[o.tok[0]] = (o.tok[1], o.tok[2])
                for key, (sem, val) in last.items():
                    if known.get(key, 0) < val:
                        h.wait_ge(sem, val)

        with nc.Block() as block:
            @block.sync
            def _(eng):
                run("sp")

            @block.scalar
            def _(eng):
                run("act")

            @block.vector
            def _(eng):
                run("dve")

            @block.gpsimd
            def _(eng):
                run("pool")

            @block.tensor
            def _(eng):
                run("pe")


def build_program(debug=()):
    nc = bass.Bass("TRN2", target_bir_lowering=False)
    S = Sched()

    def din(name, shape):
        return nc.dram_tensor(name, shape, F32, kind="ExternalInput").ap()

    def dout(name, shape):
        return nc.dram_tensor(name, shape, F32, kind="ExternalOutput").ap()

    xT_d = din("xT", [8, 128, NT])
    cst_d = din("cst", [128, NCST])
    gsub_d = din("gsub", [128, 512])
    ident_d = din("ident", [128, 128])
    perm_d = din("perm", [128, 128])
    ropeC_d = din("ropeC", [128, 1024])
    ropeS_d = din("ropeS", [128, 1024])
    kcT_d = din("kcT", [128, 4, 512])
    vc_d = din("vc", [128, 4, 512])
    wmod_d = din("wmod", [18, 128, 4096])
    up_d = [din("up1", [11, 128, 4096]), din("up2", [11, 128, 4096])]
    dn_d = [din("dn1", [4, 128, 5632]), din("dn2", [4, 128, 5632])]
    wq_d = din("wq", [128, 4096])
    wk_d = din("wk", [128, 4096])
    wv_d = din("wv", [128, 4096])
    wc_d = din("wc", [4, 128, 3072])
    wo_d = din("wo", [2, 128, 4096])
    yT_o = dout("yT", [8, 128, NT])
    kT_o = dout("kT", [4, 128, NT])
    v_o = dout("vout", [NT, 512])
    dbg_o = {}
    for name, shape in debug:
        dbg_o[name] = dout("dbg_" + name, shape)

    with ExitStack() as st:
        def sb(name, shape, dt):
            return st.enter_context(nc.sbuf_tensor(name, shape, dt))

        xT = sb("xTs", [128, 8, NT], F32)
        ring = sb("ring", [128, 3, 5632], BF16)
        R1 = sb("R1", [128, 5120], F32)
        R2 = sb("R2", [128, 14920], F32)
        R3 = sb("R3", [128, 5120], F32)
        rstd_t = sb("rstd_t", [128, 1280], F32)
        sq_t = sb("sq_t", [128, 2, 1280], BF16)
        tmp0_t = sb("tmp0_t", [128, 1280], F32)
        tmp1_t = sb("tmp1_t", [128, 1280], F32)
        cst = sb("csts", [128, NCST], F32)
        modT = sb("modT", [128, 3, 48], F32)
        acoef = sb("acoef", [128, 3, 16], F32)
        gcoef = sb("gcoef", [128, 3, 16], F32)
        ident = sb("idents", [128, 128], F32)
        perm = sb("perms", [128, 128], BF16)
        ones = sb("ones", [128, 128], BF16)
        gsub4 = sb("gsub4", [128, 4, 128], F32)
        silc = sb("silc", [128, 8, 2], BF16)
        small = sb("small", [128, 128], F32)
        ppt = sb("ppt", [128, 3, 512], F32)
        PT = sb("PT3", [128, 3, 512], BF16)
        ps = st.enter_context(nc.psum_tensor("ps", [128, 4096], F32))

        uT = R1[:].bitcast(BF16).rearrange("p (k t) -> p k t", k=8)
        hT = R2[:, 0:14080].bitcast(BF16).rearrange("p (k t) -> p k t", k=22)
        ycF = [R1[:].rearrange("p (k t) -> p k t", k=4)[:, c, :] for c in range(4)] + \
              [R3[:].rearrange("p (k t) -> p k t", k=4)[:, c, :] for c in range(4)]
        ycM = [R2[:, 0:10240].rearrange("p (k t) -> p k t", k=8)[:, c, :] for c in range(8)]
        mixT = R3[:].bitcast(BF16).rearrange("p (k t) -> p k t", k=8)
        kst = R3[:].rearrange("p (k t) -> p k t", k=4)
        QT = R2[:, 0:2560].bitcast(BF16).rearrange("p (h t) -> p h t", h=4)
        KT = R2[:, 2560:6144].bitcast(BF16).rearrange("p (h t) -> p h t", h=4)
        Vaug = R2[:, 6144:9784].bitcast(BF16).rearrange("p (t h e) -> p t h e", t=14, h=4)
        ropeC = R2[:, 10296:11320]
        ropeS = R2[:, 11320:12344]
        qbf = R2[:, 12344:12856].bitcast(BF16)
        Ocopy = R2[:, 12856:14920].rearrange("p (b m t e) -> p b m t e", b=2, m=2, t=4)
        QT1 = R2[:, 10296:12856].bitcast(BF16).rearrange("p (h t) -> p h t", h=4)
        rstd = rstd_t[:]
        Vo = R1[:, 0:2080].bitcast(BF16).rearrange("p (t h e) -> p t h e", t=8, h=4)

        class _Two:
            def __init__(self, ts):
                self.ts = ts

            def __getitem__(self, key):
                p, i, c = key
                return self.ts[i][p, c]
        sq = sq_t
        tmp = _Two([tmp0_t, tmp1_t])
        maxc = small[:, 0:28]
        mq = small[:, 28:29]
        mk = small[:, 29:30]
        negM = small[:, 30:31]
        biasAll = small[:, 32:69]
        lp = small[:, 70:72]
        le = small[:, 72:74]
        neglam = small[:, 74:75]
        nwf = small[:, 76:84].rearrange("p (a b) -> p a b", a=2)
        rs = small[:, 84:92].rearrange("p (m t) -> p m t", m=2)
        r1l = small[:, 92:96]
        ss4 = small[:, 96:100]
        rstd4 = small[:, 100:104]
        epsc = small[:, 104:105]
        sflag = small[:, 105:106]

        def cview(off, n):
            return cst[:, off:off + n]

        bmod2 = cview(C_BMOD, 144).rearrange("p (s x) -> p s x", s=3)
        gpre2 = cview(C_GPRE, 48).rearrange("p (s x) -> p s x", s=3)
        gpost2 = cview(C_GPOST, 48).rearrange("p (s x) -> p s x", s=3)
        convw = cview(C_CONVW, 12).rearrange("p (j i) -> p j i", j=4)
        cflag = cview(C_FLAG, 1)
        maskb = cview(C_MASK, 37)
        lamrow = cview(C_LAM, 256).rearrange("p (a b) -> p a b", a=4)
        condT = cview(C_COND, 16)

        PB = [Buf("pb%d" % i) for i in range(8)]
        RING = [Buf("ring%d" % i) for i in range(3)]
        XTa = [Buf("xTa%d" % i) for i in range(8)]
        XTb = [Buf("xTb%d" % i) for i in range(8)]
        XT = [Multi([XTa[i], XTb[i]]) for i in range(8)]
        UTK = [[Buf("uT%d_%d" % (i, j)) for j in range(3)] for i in range(8)]
        UT = [Multi(UTK[i]) for i in range(8)]
        HT = [Buf("hT%d" % i) for i in range(22)]
        YCF = [Buf("ycF%d" % i) for i in range(8)]
        YCM = [Buf("ycM%d" % i) for i in range(8)]
        MIX = [Buf("mix%d" % i) for i in range(8)]
        KST = [Buf("kst%d" % i) for i in range(4)]
        QTB = [Buf("QT%d" % i) for i in range(4)]
        KTB = [Buf("KT%d" % i) for i in range(4)]
        KTC = Buf("KTc")
        VA = [Buf("Vaug%d" % i) for i in range(14)]
        PTH = [[Buf("PT%d_%d" % (i, j)) for j in range(2)] for i in range(3)]
        PTB = [Multi(PTH[i]) for i in range(3)]
        ROPE = Buf("rope")
        OC = [Buf("Oc0"), Buf("Oc1")]
        QBF = Buf("qbf")
        QT1B = Buf("QT1")
        RSTD = Buf("rstd")
        SQ = [Buf("sq0"), Buf("sq1")]
        TMPK = [[Buf("tmp%d_%d" % (i, j)) for j in range(3)] for i in range(2)]
        TMP = [Multi(TMPK[i]) for i in range(2)]
        CST = Buf("cst")
        MODT = [Buf("modT%d" % i) for i in range(3)]
        COEF = [Buf("coef%d" % i) for i in range(3)]
        MODG = [Buf("modG%d" % i) for i in range(3)]
        GCOEF = [Buf("gcoef%d" % i) for i in range(3)]
        IDENT = Buf("ident")
        PERM = Buf("perm")
        ONES = Buf("ones")
        GSUB = Buf("gsub")
        SILC = Buf("silc")
        SM = Buf("small")
        SM0 = Buf("small0")
        BIASB = Buf("biasall")
        LAMB = Buf("lamb")
        SS4 = Buf("ss4")
        VOB = [Buf("Vo%d" % i) for i in range(8)]
        SFL = Buf("sflag")
        PP0, PP1, PP2 = Buf("pp0"), Buf("pp1"), Buf("pp2")
        NWF = Buf("nwf")
        R2_MIX = QTB + KTB + [KTC] + VA + [ROPE] + OC + [QBF, QT1B]

        def bank(b, n=512, off=0):
            return ps[:, 512 * b + off:512 * b + off + n]

        pieces = []
        for i in range(4):
            pieces.append((wmod_d[i], 4096))
        for j in range(11):
            pieces.append((up_d[0][j], 4096))
        for j in range(4):
            pieces.append((dn_d[0][j], 5632))
        for i in range(6, 12):
            pieces.append((wmod_d[i], 4096))
        pieces += [(wq_d, 4096), (wk_d, 4096), (wv_d, 4096)]
        for j in range(4):
            pieces.append((wc_d[j], 3072))
        pieces += [(wo_d[0], 4096), (wo_d[1], 4096)]
        for i in range(12, 18):
            pieces.append((wmod_d[i], 4096))
        for j in range(11):
            pieces.append((up_d[1][j], 4096))
        for j in range(4):
            pieces.append((dn_d[1][j], 5632))
        wstate = {"loaded": 0, "used": 0}

        def _load_piece(i):
            src, n = pieces[i]
            slot = i % 3
            extra = [XT[6]] if i in (4, 5) else []
            S.op("pool", lambda e, src=src, n=n, slot=slot: e.dma_start(out=ring[:, slot, 0:n], in_=src),
                 reads=extra, writes=[RING[slot]], dma=True)

        def prefetch(nahead):
            while wstate["loaded"] < min(len(pieces), wstate["used"] + nahead):
                _load_piece(wstate["loaded"])
                wstate["loaded"] += 1

        def next_piece(ncols_k):
            i = wstate["used"]
            wstate["used"] += 1
            while wstate["loaded"] < min(len(pieces), i + 3):
                _load_piece(wstate["loaded"])
                wstate["loaded"] += 1
            slot = i % 3
            n = pieces[i][1]
            return ring[:, slot, 0:n].rearrange("p (k c) -> p k c", k=ncols_k), RING[slot]

        S.op("sp", lambda e: e.dma_start(out=cst[:], in_=cst_d), writes=[CST], dma=True)
        S.op("sp", lambda e: e.dma_start(out=ident[:], in_=ident_d), writes=[IDENT], dma=True)
        S.op("sp", lambda e: e.dma_start(out=gsub4[:].rearrange("p a b -> p (a b)"), in_=gsub_d), writes=[GSUB], dma=True)
        for c in range(8):
            S.op("sp", lambda e, c=c: e.dma_start(out=xT[:, c, :], in_=xT_d[c]), writes=[XT[c]], dma=True)
        S.op("pool", lambda e: e.dma_start(out=perm[:], in_=perm_d), writes=[PERM], dma=True)
        S.op("pool", lambda e: e.memset(ones[:], 1.0), writes=[ONES])
        S.op("pool", lambda e: e.memset(epsc, EPS), writes=[SM0])
        S.op("act", lambda e: e.activation(out=silc[:].rearrange("p a b -> p (a b)"), in_=condT, func=AF.Silu),
             reads=[CST], writes=[SILC])

        def misc_setup():
            S.op("dve", lambda e: e.tensor_scalar(out=gsub4[:].rearrange("p a b -> p (a b)"), in0=gsub4[:].rearrange("p a b -> p (a b)"),
                                                  scalar1=1.0 - LAMBDA_INIT, scalar2=None, op0=ALU.mult),
                 reads=[GSUB], writes=[GSUB])
            S.op("dve", lambda e: e.tensor_tensor(out=ppt[:, 0, 0:128].rearrange("p (a b) -> p a b", a=2), in0=lamrow[:, 0:4:2, :],
                                                  in1=lamrow[:, 1:4:2, :], op=ALU.mult), reads=[CST], writes=[PP0])
            S.op("dve", lambda e: e.tensor_reduce(out=lp, in_=ppt[:, 0, 0:128].rearrange("p (a b) -> p a b", a=2), axis=AX.X, op=ALU.add),
                 reads=[PP0], writes=[SM])
            S.op("act", lambda e: e.activation(out=le, in_=lp, func=AF.Exp), reads=[SM], writes=[SM])
            S.op("dve", lambda e: e.tensor_tensor(out=neglam, in0=le[:, 1:2], in1=le[:, 0:1], op=ALU.subtract), reads=[SM], writes=[SM])
            S.op("dve", lambda e: e.tensor_scalar(out=neglam, in0=neglam, scalar1=-LAMBDA_INIT, scalar2=None, op0=ALU.add),
                 reads=[SM], writes=[SM, LAMB])
            S.op("dve", lambda e: e.tensor_scalar(out=nwf[:, 0, :], in0=convw[:, :, 0], scalar1=cflag, scalar2=-1.0, op0=ALU.mult, op1=ALU.mult),
                 reads=[CST], writes=[NWF])
            S.op("dve", lambda e: e.tensor_scalar(out=nwf[:, 1, :], in0=convw[:, :, 2], scalar1=cflag, scalar2=-1.0, op0=ALU.mult, op1=ALU.mult),
                 reads=[CST, NWF], writes=[NWF])

        def modulation(s, part, defer=False, hooks=None, wsrc=None):
            first = True
            for i in (range(0, 4) if part == "a" else range(4, 6)):
                if hooks is not None:
                    hooks[i]()
                if wsrc is not None:
                    w, wb = wsrc[i]
                else:
                    w, wb = next_piece(8)
                for cc in range(4):
                    ch = 4 * i + cc
                    for k in range(8):
                        S.op("pe", lambda e, w=w, cc=cc, k=k, ch=ch, f=first: e.matmul(
                            bank(7, 2, 2 * ch), lhsT=w[:, k, 128 * cc:128 * cc + 128], rhs=silc[:, k, :],
                            start=f, stop=(k == 7), skip_group_check=True),
                            reads=[wb, SILC], writes=[PB[7]])
                        first = False
            if part == "a":
                def evac_a():
                    S.op("dve", lambda e: e.tensor_tensor(out=modT[:, s, 0:32], in0=bank(7, 32), in1=bmod2[:, s, 0:32], op=ALU.add),
                         reads=[CST], excl=[PB[7]], writes=[MODT[s]])
                    S.op("dve", lambda e: e.scalar_tensor_tensor(out=acoef[:, s, :], in0=modT[:, s, 16:32], scalar=1.0, in1=gpre2[:, s, :],
                                                                 op0=ALU.add, op1=ALU.mult), reads=[MODT[s], CST], writes=[COEF[s]])
                if defer:
                    return evac_a
                evac_a()
            else:
                fac = 1.0 if s == 1 else 0.5

                def evac_b():
                    S.op("dve", lambda e: e.tensor_tensor(out=modT[:, s, 32:48], in0=bank(7, 16, 32), in1=bmod2[:, s, 32:48], op=ALU.add),
                         reads=[CST], excl=[PB[7]], writes=[MODG[s]])
                    S.op("dve", lambda e: e.scalar_tensor_tensor(out=gcoef[:, s, :], in0=modT[:, s, 32:48], scalar=fac, in1=gpost2[:, s, :],
                                                                 op0=ALU.mult, op1=ALU.mult), reads=[MODG[s], CST], writes=[GCOEF[s]])
                if defer:
                    return evac_b
                evac_b()

        def rstd_from_banks(b0, dim):
            for i, (a, b) in enumerate(BLK):
                S.op("act", lambda e, i=i, a=a, b=b: e.activation(out=rstd[:, a:b], in_=bank(b0 + i, b - a), func=AF.Ln,
                                                                  bias=epsc, scale=1.0 / dim),
                     reads=[SM0], excl=[PB[b0 + i]], writes=[RSTD])
            S.op("act", lambda e: e.activation(out=rstd, in_=rstd, func=AF.Exp, scale=-0.5), reads=[RSTD], writes=[RSTD])

        def stats_chunk(c, act_only=False):
            if c % 2 == 0 or act_only:
                S.op("act", lambda e, c=c: e.activation(out=sq[:, c % 2, :], in_=xT[:, c, :], func=AF.Square),
                     reads=[XT[c]], writes=[SQ[c % 2]])
            else:
                S.op("dve", lambda e, c=c: e.tensor_tensor(out=sq[:, c % 2, :], in0=xT[:, c, :], in1=xT[:, c, :], op=ALU.mult),
                     reads=[XT[c]], writes=[SQ[c % 2]])
            for i, (a, b) in enumerate(BLK):
                S.op("pe", lambda e, c=c, i=i, a=a, b=b: e.matmul(bank(i, b - a), lhsT=ones[:], rhs=sq[:, c % 2, a:b],
                                                                  start=(c == 0), stop=(c == 7)),
                     reads=[ONES, SQ[c % 2]], writes=[PB[i]])

        def prenorm_stats():
            for c in range(8):
                stats_chunk(c)
            rstd_from_banks(0, 1024.0)

        def prenorm_apply(s):
            for c in range(8):
                S.op("dve", lambda e, c=c: e.tensor_tensor(out=tmp[:, c % 2, 0:512], in0=xT[:, c, 0:512], in1=rstd[:, 0:512], op=ALU.mult),
                     reads=[XT[c], RSTD], writes=[TMPK[c % 2][0]])
                S.op("act", lambda e, c=c: e.activation(
                    out=uT[:, c, 0:512], in_=tmp[:, c % 2, 0:512], func=AF.Identity,
                    bias=modT[:, s, 2 * c:2 * c + 1], scale=acoef[:, s, 2 * c:2 * c + 1]),
                    reads=[TMPK[c % 2][0], MODT[s], COEF[s]], writes=[UTK[c][0]])
            for c in range(8):
                S.op("dve", lambda e, c=c: e.tensor_tensor(out=tmp[:, c % 2, 512:NT], in0=xT[:, c, 512:NT], in1=rstd[:, 512:NT], op=ALU.mult),
                     reads=[XT[c], RSTD], writes=[TMPK[c % 2][1], TMPK[c % 2][2]])
                for i in (1, 2):
                    a_, b_ = BLK[i]
                    g = 0 if i < 2 else 1
                    S.op("act", lambda e, c=c, g=g, a_=a_, b_=b_: e.activation(
                        out=uT[:, c, a_:b_], in_=tmp[:, c % 2, a_:b_], func=AF.Identity,
                        bias=modT[:, s, 2 * c + g:2 * c + g + 1], scale=acoef[:, s, 2 * c + g:2 * c + g + 1]),
                        reads=[TMPK[c % 2][i], MODT[s], COEF[s]], writes=[UTK[c][i]])

        def post_stats(m):
            for i, (a, b) in enumerate(BLK):
                S.op("pe", lambda e, m=m, i=i, a=a, b=b: e.matmul(bank(3 + i, b - a), lhsT=ones[:], rhs=sq[:, m % 2, a:b],
                                                                  start=(m == 0), stop=(m == 7)),
                     reads=[ONES, SQ[m % 2]], writes=[PB[3 + i]])

        def out_proj(s, nk, wfn, rhs, RHS, yc, YC, next_stats=True, mod_next=None):
            for m in range(8):
                w, wb, col = wfn(m)
                for i, (a, b) in enumerate(BLK):
                    for k in range(nk):
                        S.op("pe", lambda e, w=w, col=col, i=i, a=a, b=b, k=k: e.matmul(
                            bank(i, b - a), lhsT=w[:, k, col:col + 128], rhs=rhs[k][:, a:b], start=(k == 0), stop=(k == nk - 1)),
                            reads=[wb, RHS[k]], writes=[PB[i]])
                    g = 0 if i < 2 else 1
                    S.op("act", lambda e, m=m, i=i, a=a, b=b: e.activation(out=sq[:, m % 2, a:b], in_=bank(i, b - a), func=AF.Square),
                         excl=[PB[i]], writes=[SQ[m % 2]])
                    S.op("act", lambda e, m=m, i=i, a=a, b=b, g=g: e.activation(out=yc[m][:, a:b], in_=bank(i, b - a), func=AF.Copy,
                                                                               scale=gcoef[:, s, 2 * m + g:2 * m + g + 1]),
                         reads=[GCOEF[s]], excl=[PB[i]], writes=[YC[m]])
                if m >= 1:
                    post_stats(m - 1)
            post_stats(7)
            prefetch(3)
            evs = mod_next() if mod_next is not None else None
            rstd_from_banks(3, 1024.0)
            for c in range(8):
                S.op("dve", lambda e, c=c: e.tensor_tensor(out=tmp[:, c % 2, :], in0=yc[c], in1=rstd, op=ALU.mult),
                     reads=[YC[c], RSTD], writes=[TMP[c % 2]])
                S.op("dve", lambda e, c=c: e.tensor_tensor(out=xT[:, c, :], in0=xT[:, c, :], in1=tmp[:, c % 2, :], op=ALU.add),
                     reads=[TMP[c % 2], XT[c]], writes=[XT[c]])
                if next_stats:
                    stats_chunk(c, act_only=True)
            if next_stats:
                rstd_from_banks(0, 1024.0)
            return evs

        def ffn(s, mid_hook=None, last=False, mod_next=None, pre_hook=None, extra_alias=()):
            for j in range(11):
                if j == 2 and pre_hook is not None:
                    pre_hook()
                if j == 6 and mid_hook is not None:
                    mid_hook()
                w, wb = next_piece(8)
                for mm in range(2):
                    m = 2 * j + mm
                    for i, (a, b) in enumerate(BLK):
                        for k in range(8):
                            S.op("pe", lambda e, w=w, mm=mm, i=i, a=a, b=b, k=k: e.matmul(
                                bank(i, b - a), lhsT=w[:, k, 128 * mm:128 * mm + 128], rhs=uT[:, k, a:b], start=(k == 0), stop=(k == 7)),
                                reads=[wb, UTK[k][i]], writes=[PB[i]])
                        for k in range(8):
                            S.op("pe", lambda e, w=w, mm=mm, i=i, a=a, b=b, k=k: e.matmul(
                                bank(3 + i, b - a), lhsT=w[:, k, 256 + 128 * mm:256 + 128 * mm + 128], rhs=uT[:, k, a:b],
                                start=(k == 0), stop=(k == 7)),
                                reads=[wb, UTK[k][i]], writes=[PB[3 + i]])
                        S.op("act", lambda e, m=m, i=i, a=a, b=b: e.activation(out=tmp[:, m % 2, a:b], in_=bank(i, b - a), func=AF.Silu),
                             excl=[PB[i]], writes=[TMP[m % 2]])
                        S.op("dve", lambda e, m=m, i=i, a=a, b=b: e.tensor_tensor(out=hT[:, m, a:b], in0=bank(3 + i, b - a),
                                                                                  in1=tmp[:, m % 2, a:b], op=ALU.mult),
                             reads=[TMP[m % 2]], excl=[PB[3 + i]], writes=[HT[m]])
            S.alias(YCF, UT + MIX + KST + list(extra_alias))
            dstate = {}

            def wfn(m):
                if m % 2 == 0:
                    dstate["w"] = next_piece(22)
                w, wb = dstate["w"]
                return w, wb, 128 * (m % 2)

            evs = out_proj(s, 22, wfn, [hT[:, k, :] for k in range(22)], HT, ycF, YCF, next_stats=not last, mod_next=mod_next)
            S.alias(UT + MIX + KST, YCF)
            return evs

        hooks0 = {0: lambda: None,
                  1: lambda: [stats_chunk(c) for c in (0, 1)],
                  2: lambda: [stats_chunk(c) for c in (2, 3)],
                  3: lambda: [stats_chunk(c) for c in (4, 5, 6, 7)]}
        ev_ = modulation(0, "a", defer=True, hooks=hooks0)
        rstd_from_banks(0, 1024.0)
        ev_()
        prenorm_apply(0)
        misc_setup()
        modb_w = R3[:, 0:4096].bitcast(BF16).rearrange("p (i k c) -> p i k c", i=2, k=8)
        MODBW = [Buf("modbw0"), Buf("modbw1")]
        S.alias(MODBW, MIX + KST)

        def load_modb():
            for ii in range(2):
                S.op("pool", lambda e, ii=ii: e.dma_start(out=modb_w[:, ii].rearrange("p k c -> p (k c)"), in_=wmod_d[4 + ii]),
                     writes=[MODBW[ii]], dma=True)
        modb_src = {4: (modb_w[:, 0], MODBW[0]), 5: (modb_w[:, 1], MODBW[1])}
        evs = ffn(0, mid_hook=lambda: modulation(0, "b", wsrc=modb_src), pre_hook=load_modb, extra_alias=MODBW,
                  mod_next=lambda: (modulation(1, "a", defer=True), modulation(1, "b", defer=True)))
        evs[0]()
        prenorm_apply(1)
        evs[1]()

        S.alias(R2_MIX, HT)
        S.op("sp", lambda e: e.dma_start(out=ropeC, in_=ropeC_d), writes=[ROPE], dma=True)
        S.op("sp", lambda e: e.dma_start(out=ropeS, in_=ropeS_d), writes=[ROPE], dma=True)
        S.op("pool", lambda e: e.dma_start(out=KT[:, :, 1280:1792], in_=kcT_d), writes=[KTC], dma=True)
        for i in range(4):
            S.op("pool", lambda e, i=i: e.dma_start(out=Vaug[:, 10 + i, :, 0:128], in_=vc_d[:, i, :].rearrange("p (h e) -> p h e", h=4)),
                 writes=[VA[10 + i]], dma=True)
        for t in range(14):
            S.op("dve", lambda e, t=t: e.memset(Vaug[:, t, :, 128:129], 1.0), writes=[VA[t]])

        def norm_max(src_ap, SRC, n, col):
            S.op("pe", lambda e: e.matmul(bank(7, n), lhsT=ones[:], rhs=src_ap, start=True, stop=True),
                 reads=[ONES, SRC], writes=[PB[7]])
            S.op("dve", lambda e: e.reduce_max(out=maxc[:, col:col + 1], in_=bank(7, n), axis=AX.X), excl=[PB[7]], writes=[SM])

        deferred = []

        def flush_deferred():
            while deferred:
                deferred.pop(0)()

        def qk_proj(dstT, DST, is_k, colbase):
            w, wb = next_piece(8)
            for i, (a, b) in enumerate(BLK):
                for h in range(4):
                    base = 0 if h % 2 == 0 else 3
                    pb = PB[base + i]
                    for k in range(8):
                        S.op("pe", lambda e, w=w, h=h, i=i, a=a, b=b, k=k, base=base: e.matmul(
                            bank(base + i, b - a), lhsT=w[:, k, 128 * h:128 * h + 128], rhs=uT[:, k, a:b], start=(k == 0), stop=(k == 7)),
                            reads=[wb, UTK[k][i]], writes=[pb])
                    flush_deferred()
                    src = bank(base + i, b - a)
                    sqi = (h * 3 + i) % 2
                    if i < 2:
                        S.op("act", lambda e, src=src, a=a, b=b: e.activation(out=qbf[:, a:b], in_=src, func=AF.Copy),
                             excl=[pb], writes=[QBF])
                    else:
                        S.op("act", lambda e, src=src, h=h, a=a, b=b: e.activation(out=dstT[:, h, a:b], in_=src, func=AF.Copy),
                             excl=[pb], writes=[DST[h]])
                    S.op("act", lambda e, src=src, sqi=sqi, a=a, b=b: e.activation(out=sq[:, sqi, 0:b - a], in_=src, func=AF.Square),
                         excl=[pb], writes=[SQ[sqi]])
                    if is_k:
                        S.op("act", lambda e, src=src, h=h, a=a, b=b: e.activation(out=kst[:, h, a:b], in_=src, func=AF.Copy),
                             excl=[pb], writes=[KST[h]])
                    if i < 2:
                        S.op("dve", lambda e, src=src, a=a, b=b: e.tensor_tensor(out=tmp[:, 0, a:b], in0=src, in1=ropeC[:, a:b], op=ALU.mult),
                             reads=[ROPE], excl=[pb], writes=[TMP[0]])

                        def rope_tail(h=h, a=a, b=b):
                            S.op("pe", lambda e: e.matmul(bank(6, 512), lhsT=perm[:], rhs=qbf[:, a:b], start=True, stop=True),
                                 reads=[PERM, QBF], writes=[PB[6]])
                            S.op("dve", lambda e: e.tensor_tensor(out=tmp[:, 1, a:b], in0=bank(6, 512), in1=ropeS[:, a:b], op=ALU.mult),
                                 reads=[ROPE], excl=[PB[6]], writes=[TMP[1]])
                            S.op("dve", lambda e: e.tensor_tensor(out=dstT[:, h, a:b], in0=tmp[:, 0, a:b], in1=tmp[:, 1, a:b], op=ALU.add),
                                 reads=[TMP[0], TMP[1]], writes=[DST[h]])
                        deferred.append(rope_tail)
                    deferred.append(lambda sqi=sqi, a=a, b=b, col=colbase + h * 3 + i: norm_max(sq[:, sqi, 0:b - a], SQ[sqi], b - a, col))
            if is_k:
                for h in range(4):
                    S.op("sp", lambda e, h=h: e.dma_start(out=kT_o[h], in_=kst[:, h, :]), reads=[KST[h]], dma=True)

        S.alias(KST, MIX)
        qk_proj(QT, QTB, False, 0)
        qk_proj(KT, KTB, True, 12)
        flush_deferred()
        S.alias([QT1B], [ROPE, QBF])
        S.op("pool", lambda e: e.memset(QT1[0:64, :, :], 0.0), writes=[QT1B])
        for h in range(4):
            S.op("pool", lambda e, h=h: e.tensor_copy(out=QT1[64:128, h, :], in_=QT[64:128, h, :]), reads=[QTB[h]], writes=[QT1B])
        for h in range(4):
            S.op("pool", lambda e, h=h: e.memset(QT[64:128, h, :], 0.0), reads=[QT1B], writes=[QTB[h]])
        for h in range(4):
            S.op("act", lambda e, h=h: e.activation(out=sq[:, h % 2, 0:512], in_=KT[:, h, 1280:1792], func=AF.Square),
                 reads=[KTC], writes=[SQ[h % 2]])
            norm_max(sq[:, h % 2, 0:512], SQ[h % 2], 512, 24 + h)
        S.op("dve", lambda e: e.reduce_max(out=mq, in_=maxc[:, 0:12], axis=AX.X), reads=[SM], writes=[SM])
        S.op("dve", lambda e: e.reduce_max(out=mk, in_=maxc[:, 12:28], axis=AX.X), reads=[SM], writes=[SM])
        S.op("dve", lambda e: e.tensor_tensor(out=negM, in0=mq, in1=mk, op=ALU.add), reads=[SM], writes=[SM])
        S.op("dve", lambda e: e.tensor_scalar(out=negM, in0=negM, scalar1=-0.5 * 0.125, scalar2=None, op0=ALU.mult), reads=[SM], writes=[SM])
        S.op("dve", lambda e: e.tensor_scalar(out=biasAll, in0=maskb, scalar1=negM, scalar2=None, op0=ALU.add), reads=[SM, CST], writes=[BIASB])

        w, wb = next_piece(8)
        for t in range(10):
            bk = t % 6
            for k in range(8):
                S.op("pe", lambda e, w=w, t=t, k=k, bk=bk: e.matmul(bank(bk, 512), lhsT=uT[:, k, 128 * t:128 * t + 128], rhs=w[:, k, :],
                                                                   start=(k == 0), stop=(k == 7)),
                     reads=[wb, UTK[k][min(t // 4, 2)]], writes=[PB[bk]])
            vs = tmp[:, t % 2, 0:512]
            S.op("act", lambda e, vs=vs, bk=bk: e.activation(out=vs, in_=bank(bk, 512), func=AF.Copy), excl=[PB[bk]], writes=[TMP[t % 2]])
            S.op("sp", lambda e, vs=vs, t=t: e.dma_start(out=v_o[128 * t:128 * t + 128, :], in_=vs), reads=[TMP[t % 2]], dma=True)
            S.op("dve", lambda e, vs=vs, t=t: e.tensor_copy(out=Vaug[:, t, :, 0:128], in_=vs.rearrange("p (h e) -> p h e", h=4)),
                 reads=[TMP[t % 2]], writes=[VA[t]])

        S.alias(MIX, KST)

        conv_state = {"next": 6, "open": False}

        def conv_gen():
            zb = tmp[:, 0, :]
            y = tmp[:, 1, :]
            bgS = sq_t[:].rearrange("p a b -> p (a b)").bitcast(F32)
            s0 = rstd
            s2 = ppt[:].rearrange("p a b -> p (a b)")[:, 0:NT]
            PPALL = Multi([PP0, PP1, PP2])
            CBANKS = [2, 3, 4, 5, 6]
            unit = 0
            for j in range(4):
                w, wb = next_piece(8)
                for typ in range(3):
                    for i, (a, b) in enumerate(BLK):
                        cb = CBANKS[unit % 5]
                        unit += 1
                        conv_state["open"] = True
                        for k in range(8):
                            S.op("pe", lambda e, w=w, typ=typ, cb=cb, a=a, b=b, k=k: e.matmul(
                                bank(cb, b - a), lhsT=w[:, k, 128 * typ:128 * typ + 128], rhs=uT[:, k, a:b], start=(k == 0), stop=(k == 7)),
                                reads=[wb, UTK[k][i]], writes=[PB[cb]])
                            if k < 7:
                                yield
                        if typ == 0:
                            S.op("act", lambda e, cb=cb, a=a, b=b: e.activation(out=zb[:, a:b], in_=bank(cb, b - a), func=AF.Copy),
                                 excl=[PB[cb]], writes=[TMP[0]])
                        elif typ == 1:
                            S.op("dve", lambda e, cb=cb, a=a, b=b: e.tensor_tensor(out=zb[:, a:b], in0=bank(cb, b - a), in1=zb[:, a:b], op=ALU.mult),
                                 reads=[TMP[0]], excl=[PB[cb]], writes=[TMP[0]])
                        else:
                            S.op("act", lambda e, cb=cb, a=a, b=b: e.activation(out=bgS[:, a:b], in_=bank(cb, b - a), func=AF.Copy),
                                 excl=[PB[cb]], writes=[SQ[0], SQ[1]])
                        conv_state["open"] = False
                        yield
                    if typ == 1:
                        z = zb
                        S.op("act", lambda e, j=j: e.activation(out=y, in_=z, func=AF.Copy, scale=convw[:, j, 1:2]),
                             reads=[TMP[0], CST], writes=[TMP[1]])
                        S.op("act", lambda e, j=j: e.activation(out=s0, in_=z, func=AF.Copy, scale=convw[:, j, 0:1]),
                             reads=[TMP[0], CST], writes=[RSTD])
                        S.op("act", lambda e, j=j: e.activation(out=s2, in_=z, func=AF.Copy, scale=convw[:, j, 2:3]),
                             reads=[TMP[0], CST], writes=[PPALL])
                        for (lo, hi) in GRP:
                            S.op("dve", lambda e, lo=lo, hi=hi: e.tensor_tensor(out=y[:, lo + 1:hi], in0=y[:, lo + 1:hi], in1=s0[:, lo:hi - 1], op=ALU.add),
                                 reads=[RSTD, TMP[1]], writes=[TMP[1]])
                            S.op("dve", lambda e, lo=lo, hi=hi: e.tensor_tensor(out=y[:, lo:hi - 1], in0=y[:, lo:hi - 1], in1=s2[:, lo + 1:hi], op=ALU.add),
                                 reads=[PPALL, TMP[1]], writes=[TMP[1]])
                        S.op("dve", lambda e, j=j: e.scalar_tensor_tensor(out=y[:, 256:1024:256], in0=z[:, 255:1023:256], scalar=nwf[:, 0, j:j + 1],
                                                                          in1=y[:, 256:1024:256], op0=ALU.mult, op1=ALU.add),
                             reads=[TMP[0], NWF, TMP[1]], writes=[TMP[1]])
                        S.op("dve", lambda e, j=j: e.scalar_tensor_tensor(out=y[:, 255:1023:256], in0=z[:, 256:1024:256], scalar=nwf[:, 1, j:j + 1],
                                                                          in1=y[:, 255:1023:256], op0=ALU.mult, op1=ALU.add),
                             reads=[TMP[0], NWF, TMP[1]], writes=[TMP[1]])
                S.op("dve", lambda e, j=j: e.tensor_tensor(out=mixT[:, 4 + j, :], in0=bgS, in1=y, op=ALU.mult),
                     reads=[TMP[1], SQ[0], SQ[1]], writes=[MIX[4 + j]])

        o_t = ppt[:, 0, :].rearrange("p (t e) -> p t e", t=4)
        t1_t = ppt[:, 1, :].rearrange("p (t e) -> p t e", t=4)
        on_t = ppt[:, 2, :].rearrange("p (t e) -> p t e", t=4)
        osq_t = t1_t
        pp_state = {"n": 0, "ob": 0}

        pending = []

        def postproc(ob, nt, h, tok0, n):
            O = Ocopy[:, ob]
            OB = OC[ob]
            pp_state["n"] += 1
            S.op("dve", lambda e: e.reciprocal(out=rs[:, :, 0:nt], in_=O[:, :, 0:nt, 128]), reads=[OB], writes=[SM])
            S.op("dve", lambda e: e.tensor_scalar(out=r1l[:, 0:nt], in0=rs[:, 1, 0:nt], scalar1=neglam, scalar2=None, op0=ALU.mult),
                 reads=[SM, LAMB], writes=[SM])
            S.op("dve", lambda e: e.tensor_tensor(out=o_t[:, 0:nt, :], in0=O[:, 0, 0:nt, 0:128],
                                                  in1=rs[:, 0, 0:nt].unsqueeze(2).to_broadcast([128, nt, 128]), op=ALU.mult),
                 reads=[OB, SM], writes=[PP0])
            S.op("dve", lambda e: e.tensor_tensor(out=t1_t[:, 0:nt, :], in0=O[:, 1, 0:nt, 0:128],
                                                  in1=r1l[:, 0:nt].unsqueeze(2).to_broadcast([128, nt, 128]), op=ALU.mult),
                 reads=[OB, SM], writes=[PP1])
            S.op("dve", lambda e: e.tensor_tensor(out=o_t[:, 0:nt, :], in0=o_t[:, 0:nt, :], in1=t1_t[:, 0:nt, :], op=ALU.add),
                 reads=[PP0, PP1], writes=[PP0])
            S.op("dve", lambda e: e.tensor_tensor(out=osq_t[:, 0:nt, :], in0=o_t[:, 0:nt, :], in1=o_t[:, 0:nt, :], op=ALU.mult),
                 reads=[PP0], writes=[PP1])
            S.op("dve", lambda e: e.tensor_reduce(out=ss4[:, 0:nt], in_=osq_t[:, 0:nt, :], axis=AX.X, op=ALU.add),
                 reads=[PP1], writes=[SS4])

            def stage2():
                S.op("act", lambda e: e.activation(out=ss4[:, 0:nt], in_=ss4[:, 0:nt], func=AF.Ln, bias=epsc, scale=1.0 / 128.0),
                     reads=[SS4, SM0], writes=[SS4])
                S.op("act", lambda e: e.activation(out=rstd4[:, 0:nt], in_=ss4[:, 0:nt], func=AF.Exp, scale=-0.5),
                     reads=[SS4], writes=[SS4])
                S.op("dve", lambda e: e.tensor_tensor(out=on_t[:, 0:nt, :], in0=o_t[:, 0:nt, :],
                                                      in1=rstd4[:, 0:nt].unsqueeze(2).to_broadcast([128, nt, 128]), op=ALU.mult),
                     reads=[PP0, SS4], writes=[PP2])
                S.op("dve", lambda e: e.tensor_tensor(out=on_t[:, 0:nt, :], in0=on_t[:, 0:nt, :], in1=gsub4[:, 0:nt, :], op=ALU.mult),
                     reads=[PP2, GSUB], writes=[PP2])

            def stage3():
                while conv_state["open"]:
                    next(cg_, None)
                tb = 6
                for t in range(nt):
                    S.op("pe", lambda e, t=t: e.transpose(out=bank(tb, 128, 128 * t), in_=on_t[:, t, :], identity=ident[:]),
                         reads=[PP2, IDENT], writes=[PB[tb]])
                S.op("dve", lambda e: e.tensor_copy(out=mixT[:, h, tok0:tok0 + 128 * nt], in_=bank(tb, 128 * nt)),
                     excl=[PB[tb]], writes=[MIX[h]])
            d2, d3 = (8, 12) if (nt == 4 and n + 13 < 192) else (3, 6)
            pending.append((n + d2, stage2))
            pending.append((n + d3, stage3))

        def run_pending(n):
            while pending and pending[0][0] <= n:
                pending.pop(0)[1]()

        SBANKS = [0, 1, 7]
        iters = []
        for h in range(4):
            for qb in range(2):
                for m in range(2):
                    for ci in range(12):
                        if ci < 8:
                            chunk = (128 * ci, ci, [36])
                        else:
                            chunk = (1280 + 128 * (ci - 8), 10 + ci - 8, [36])
                        iters.append((h, m, 512 * qb, 512, 4, ci, 12, chunk))
        for h in range(4):
            for m in range(2):
                for ci in range(2):
                    iters.append((h, m, 1024, 256, 2, ci, 2, (1024 + 128 * ci, 8 + ci, [36])))

        def emit_S(n):
            h, m, q0, nq, nt, ci, nch, (kc0, vt, bcols) = iters[n]
            sb_ = SBANKS[n % 3]
            Qm = QT if m == 0 else QT1
            S.op("pe", lambda e: e.matmul(bank(sb_, nq), lhsT=KT[:, h, kc0:kc0 + 128],
                                          rhs=Qm[:, h, q0:q0 + nq], start=True, stop=True),
                 reads=[KTB[h], KTC, QTB[h], QT1B], writes=[PB[sb_]])

        def emit_exp_pv(n):
            h, m, q0, nq, nt, ci, nch, (kc0, vt, bcols) = iters[n]
            sb_ = SBANKS[n % 3]
            pt_ = n % 3
            ob0 = 2 if m == 0 else 4
            seg = nq // len(bcols)
            for si, bc in enumerate(bcols):
                S.op("act", lambda e, si=si, bc=bc: e.activation(
                    out=PT[:, pt_, si * seg:(si + 1) * seg], in_=bank(sb_, seg, si * seg), func=AF.Exp,
                    bias=biasAll[:, bc:bc + 1], scale=0.125),
                    reads=[BIASB], excl=[PB[sb_]], writes=[PTH[pt_][si] if len(bcols) == 2 else PTB[pt_]])
            for t in range(nt):
                bb = t // 2
                if nt == 4 and vt < 8 and (q0 + 128 * t) // 256 != vt // 2:
                    vsrc, VB = Vo[:, vt, h, 0:129], VOB[vt]
                else:
                    vsrc, VB = Vaug[:, vt, h, 0:129], VA[vt]
                S.op("pe", lambda e, t=t, bb=bb, vsrc=vsrc: e.matmul(
                    bank(ob0 + bb, 129, 129 * (t % 2)), lhsT=PT[:, pt_, 128 * t:128 * t + 128], rhs=vsrc,
                    start=(ci == 0 and t % 2 == 0), stop=(ci == nch - 1), skip_group_check=True),
                    reads=[PTB[pt_], VB], writes=[PB[ob0 + bb]])
            if ci == nch - 1:
                ob = pp_state["ob"]
                nbank = (nt + 1) // 2
                for bb in range(nbank):
                    ntb = min(2, nt - 2 * bb)
                    S.op("dve", lambda e, bb=bb, ntb=ntb: e.tensor_copy(
                        out=Ocopy[:, ob, m, 2 * bb:2 * bb + ntb, :], in_=bank(ob0 + bb, 129 * ntb).rearrange("p (t e) -> p t e", t=ntb)),
                        excl=[PB[ob0 + bb]], writes=[OC[ob]])
                if m == 1:
                    postproc(ob, nt, h, q0, n)
                    pp_state["ob"] = 1 - ob

        prefetch(2)
        cg_ = conv_gen()
        for _ in cg_:
            pass
        S.alias(VOB, UT)
        S.op("dve", lambda e: e.tensor_scalar(out=sflag, in0=cflag, scalar1=-1.0, scalar2=1.0, op0=ALU.mult, op1=ALU.add),
             reads=[CST], writes=[SFL])
        for t in range(8):
            S.op("act", lambda e, t=t: e.activation(out=Vo[:, t, :, :].rearrange("p h e -> p (h e)"),
                                                     in_=Vaug[:, t, :, :].rearrange("p h e -> p (h e)"),
                                                     func=AF.Copy, scale=sflag),
                 reads=[VA[t], SFL], writes=[VOB[t]])
        for t in range(10, 14):
            S.op("act", lambda e, t=t: e.activation(out=Vaug[:, t, :, :].rearrange("p h e -> p (h e)"),
                                                     in_=Vaug[:, t, :, :].rearrange("p h e -> p (h e)"),
                                                     func=AF.Copy, scale=sflag),
                 reads=[VA[t], SFL], writes=[VA[t]])
        emit_S(0)
        emit_S(1)
        for n in range(len(iters)):
            if n + 2 < len(iters):
                emit_S(n + 2)
            emit_exp_pv(n)
            run_pending(n)
        run_pending(10 ** 9)
        for _ in cg_:
            pass

        S.alias(UT, VOB)
        S.alias(YCM, R2_MIX)
        wostate = {}

        def wofn(m):
            if m % 4 == 0:
                wostate["w"] = next_piece(8)
            w, wb = wostate["w"]
            return w, wb, 128 * (m % 4)

        evs = out_proj(1, 8, wofn, [mixT[:, k, :] for k in range(8)], MIX, ycM, YCM,
                       mod_next=lambda: (modulation(2, "a", defer=True), modulation(2, "b", defer=True)))
        S.alias(HT, YCM + R2_MIX)
        evs[0]()
        prenorm_apply(2)
        evs[1]()
        ffn(2, last=True)

        for c in range(8):
            S.op("sp", lambda e, c=c: e.dma_start(out=yT_o[c], in_=xT[:, c, :]), reads=[XT[c]], dma=True)

        S.emit(nc, st)
    return nc


def _rope_tables():
    t = np.arange(1024)
    row = (t // 64).astype(np.float32)
    col = (t % 64).astype(np.float32)
    half = 32
    freqs = (10000.0 ** (-np.arange(0, half, 2, dtype=np.float32) / half)).astype(np.float32)
    C = np.zeros((128, 1024), np.float32)
    Sg = np.zeros((128, 1024), np.float32)
    P = np.zeros((128, 128), np.float32)
    for p in range(128):
        d = p % 64
        pos = row if d < 32 else col
        dd = d % 32
        f = freqs[dd % 16]
        ang = (pos * f).astype(np.float32)
        C[p] = np.cos(ang)
        if dd < 16:
            Sg[p] = -np.sin(ang)
            partner = p + 16
        else:
            Sg[p] = np.sin(ang)
            partner = p - 16
        P[partner, p] = 1.0
    return C, Sg, P


def _pieces_cols(W, col_lists):
    K = W.shape[0] // 128
    out = []
    for cols in col_lists:
        sub = W[:, cols]
        sub = sub.reshape(K, 128, len(cols)).transpose(1, 0, 2)
        out.append(np.ascontiguousarray(sub).reshape(128, K * len(cols)))
    return np.stack(out, 0)


_PROG = {}


def kernel(x_prompt, x_sample, c, cache_k, cache_v, c_ctx, w_mod, b_mod, norm_pre, norm_post,
           ffn1_up, ffn1_down, ffn2_up, ffn2_down, w_in, conv_w, lam_qk, subln_g, w_o):
    f = lambda a: np.ascontiguousarray(np.asarray(a, dtype=np.float32))
    x_prompt, x_sample, c, cache_k, cache_v, c_ctx = map(f, (x_prompt, x_sample, c, cache_k, cache_v, c_ctx))
    w_mod, b_mod, norm_pre, norm_post = f(w_mod)[0], f(b_mod)[0], f(norm_pre)[0], f(norm_post)[0]
    ffn1_up, ffn1_down, ffn2_up, ffn2_down = f(ffn1_up)[0], f(ffn1_down)[0], f(ffn2_up)[0], f(ffn2_down)[0]
    w_in, conv_w, lam_qk, subln_g, w_o = f(w_in)[0], f(conv_w)[0], f(lam_qk)[0], f(subln_g)[0], f(w_o)[0]

    ar = np.arange
    wmodP = _pieces_cols(w_mod, [ar(512 * i, 512 * i + 512) for i in range(18)])
    upcols = [np.concatenate([ar(256 * j, 256 * j + 256), ar(2816 + 256 * j, 2816 + 256 * j + 256)]) for j in range(11)]
    up1P = _pieces_cols(ffn1_up, upcols)
    up2P = _pieces_cols(ffn2_up, upcols)
    dn1P = _pieces_cols(ffn1_down, [ar(256 * j, 256 * j + 256) for j in range(4)])
    dn2P = _pieces_cols(ffn2_down, [ar(256 * j, 256 * j + 256) for j in range(4)])
    wqP = _pieces_cols(w_in, [ar(0, 512)])[0]
    wkP = _pieces_cols(w_in, [ar(512, 1024)])[0]
    wvP = _pieces_cols(w_in, [ar(1024, 1536)])[0]
    wcP = _pieces_cols(w_in, [np.concatenate([ar(2560 + 128 * j, 2560 + 128 * j + 128), ar(2048 + 128 * j, 2048 + 128 * j + 128),
                                              ar(1536 + 128 * j, 1536 + 128 * j + 128)]) for j in range(4)])
    woP = _pieces_cols(w_o, [ar(0, 512), ar(512, 1024)])

    ropeC, ropeS, perm = _rope_tables()
    ident = np.eye(128, dtype=np.float32)
    gsub = np.ascontiguousarray(np.broadcast_to(np.tile(subln_g, 4)[None, :], (128, 512))).astype(np.float32)

    def tmaj(v):
        return np.ascontiguousarray(v.reshape(-1, 128).T)

    shared_cst = np.zeros((128, NCST), np.float32)
    bm = tmaj(b_mod)
    shared_cst[:, C_BMOD:C_BMOD + 144] = np.repeat(bm, 2, axis=1)
    gp = np.stack([tmaj(norm_pre[s]) for s in range(3)], 1)
    shared_cst[:, C_GPRE:C_GPRE + 48] = np.repeat(gp.reshape(128, 24), 2, axis=1)
    gq = np.stack([tmaj(norm_post[s]) for s in range(3)], 1)
    shared_cst[:, C_GPOST:C_GPOST + 48] = np.repeat(gq.reshape(128, 24), 2, axis=1)
    cw = np.stack([tmaj(conv_w[i]) for i in range(3)], 2)
    shared_cst[:, C_CONVW:C_CONVW + 12] = cw.reshape(128, 12)
    shared_cst[:, C_LAM:C_LAM + 256] = lam_qk.reshape(1, 256)

    in_maps = []
    groups = []
    for core in range(8):
        if core < 6:
            tokA = x_prompt[5 * core:5 * core + 4].reshape(1024, 1024)
            tokB = x_prompt[5 * core + 4]
            condA = c_ctx
            sample = None
        else:
            sample = core - 6
            tokA = x_sample[sample]
            tokB = x_prompt[30 + sample]
            condA = c[sample]
        tok = np.concatenate([tokA, tokB], 0)
        xT = np.ascontiguousarray(tok.T).reshape(8, 128, NT)
        cst = shared_cst.copy()
        cond2 = np.stack([tmaj(condA), tmaj(c_ctx)], 2)
        cst[:, C_COND:C_COND + 16] = cond2.reshape(128, 16)
        mask = np.zeros(37, np.float32)
        if sample is None:
            for cc in range(8):
                for j in range(4):
                    mask[cc * 4 + j] = 0.0 if (cc // 2) == j else NEG
            mask[32:36] = NEG
            cst[:, C_FLAG] = 1.0
            rc, rs_ = np.ones_like(ropeC), np.zeros_like(ropeS)
            cb = 0
        else:
            rc, rs_ = ropeC, ropeS
            cb = sample
        cst[:, C_MASK:C_MASK + 37] = mask[None, :]
        ck = cache_k[cb, 0]
        kcT = np.ascontiguousarray(ck.transpose(2, 1, 0))
        cv = cache_v[cb, 0].reshape(4, 128, 512)
        vc = np.ascontiguousarray(cv.transpose(1, 0, 2))
        in_maps.append({
            "xT": xT, "cst": cst, "gsub": gsub, "ident": ident, "perm": perm, "ropeC": rc, "ropeS": rs_,
            "kcT": kcT, "vc": vc, "wmod": wmodP, "up1": up1P, "up2": up2P, "dn1": dn1P, "dn2": dn2P,
            "wq": wqP, "wk": wkP, "wv": wvP, "wc": wcP, "wo": woP,
        })

    if "nc" not in _PROG:
        _PROG["nc"] = build_program()
    res = run_bass_kernel_spmd(_PROG["nc"], in_maps, core_ids=list(range(8)))

    y_prompt = np.zeros((32, 256, 1024), np.float32)
    y_sample = np.zeros((2, 1024, 1024), np.float32)
    new_k = np.zeros((32, 1, 256, 4, 128), np.float32)
    new_v = np.zeros((32, 1, 256, 4, 128), np.float32)
    for core in range(8):
        r = res.results[core]
        y = r["yT"].reshape(1024, NT).T
        k = r["kT"].reshape(4, 128, NT).transpose(2, 0, 1)
        v = r["vout"].reshape(NT, 4, 128)
        if core < 6:
            for i in range(4):
                y_prompt[5 * core + i] = y[256 * i:256 * i + 256]
                new_k[5 * core + i, 0] = k[256 * i:256 * i + 256]
                new_v[5 * core + i, 0] = v[256 * i:256 * i + 256]
            bB = 5 * core + 4
        else:
            y_sample[core - 6] = y[0:1024]
            bB = 30 + core - 6
        y_prompt[bB] = y[1024:1280]
        new_k[bB, 0] = k[1024:1280]
        new_v[bB, 0] = v[1024:1280]
    return (y_prompt, y_sample, new_k, new_v)
```

```python
import math
import numpy as np
import concourse.bass as bass
import concourse.mybir as mybir
from concourse.bass_utils import run_bass_kernel_spmd
from contextlib import ExitStack

F32 = mybir.dt.float32
BF16 = mybir.dt.bfloat16
AF = mybir.ActivationFunctionType
ALU = mybir.AluOpType
AX = mybir.AxisListType

DMA_K = 12
NT = 1280
BLK = [(0, 512), (512, 1024), (1024, 1280)]
GRP = [(0, 1024), (1024, 1280)]
EPS = 1e-6
NEG = -30000.0
SPL = 1024
LAMBDA_INIT = 0.2
C_BMOD = 0
C_GPRE = 144
C_GPOST = 192
C_CONVW = 240
C_FLAG = 252
C_MASK = 253
C_LAM = 290
C_COND = 546
NCST = 562


class Buf:
    __slots__ = ("name", "lw", "rd")

    def __init__(self, name):
        self.name = name
        self.lw = None
        self.rd = {}


class Multi(list):
    pass


def _flat(bufs):
    out = []
    for b in bufs:
        if isinstance(b, Multi):
            out.extend(b)
        else:
            out.append(b)
    return out


class Op:
    __slots__ = ("eng", "fn", "deps", "needed", "tok", "is_dma", "qidx")

    def __init__(self, eng, fn, is_dma):
        self.eng = eng
        self.fn = fn
        self.deps = []
        self.needed = False
        self.tok = None
        self.is_dma = is_dma
        self.qidx = -1


class Sched:
    ENGS = ("pe", "act", "dve", "pool", "sp")

    def __init__(self):
        self.prog = {e: [] for e in self.ENGS}
        self.dmas = {e: [] for e in self.ENGS}

    def _dep(self, op, prod, raw, cross_only=False):
        if prod is None or prod is op:
            return
        if (not prod.is_dma) and prod.eng == op.eng and not op.is_dma:
            if op.eng == "pe" or cross_only:
                return
        prod.needed = True
        op.deps.append(prod)

    def op(self, eng, fn, reads=(), writes=(), dma=False, excl=()):
        o = Op(eng, fn, dma)
        reads = _flat(reads)
        writes = _flat(writes)
        for b in excl:
            self._dep(o, b.lw, False, cross_only=True)
            for r in b.rd.values():
                self._dep(o, r, False, cross_only=True)
        for b in reads:
            self._dep(o, b.lw, True)
        for b in writes:
            self._dep(o, b.lw, False)
            for r in b.rd.values():
                self._dep(o, r, False)
        if dma:
            q = self.dmas[eng]
            o.qidx = len(q)
            if o.qidx >= DMA_K:
                prev = q[o.qidx - DMA_K]
                prev.needed = True
                o.deps.append(prev)
            q.append(o)
            o.needed = True
        for b in reads:
            key = id(o) if dma else eng
            b.rd[key] = o
        for b in writes:
            b.lw = o
            b.rd = {}
        for b in excl:
            b.lw = o
            b.rd = {}
        self.prog[eng].append(o)
        return o

    def alias(self, new_bufs, old_bufs):
        new_bufs = _flat(new_bufs)
        old_bufs = _flat(old_bufs)
        users = {}
        for ob in old_bufs:
            if ob.lw is not None:
                users[id(ob.lw)] = ob.lw
            for r in ob.rd.values():
                users[id(r)] = r
        for nb in new_bufs:
            if nb.lw is not None:
                users[id(nb.lw)] = nb.lw
            for r in nb.rd.values():
                users[id(r)] = r
        for nb in new_bufs:
            nb.lw = None
            nb.rd = dict(users)

    def emit(self, nc, stack):
        sems = {}
        for e in self.ENGS:
            sems[e] = stack.enter_context(nc.semaphore("s_" + e))
        dsems = {}
        for e in self.ENGS:
            if self.dmas[e]:
                dsems[e] = [stack.enter_context(nc.semaphore("d_%s%d" % (e, i))) for i in range(DMA_K)]
        for e in self.ENGS:
            cnt = 0
            for o in self.prog[e]:
                if o.is_dma:
                    o.tok = (("d", e, o.qidx % DMA_K), dsems[e][o.qidx % DMA_K], 16 * (o.qidx // DMA_K + 1))
                elif o.needed:
                    cnt += 1
                    o.tok = (("c", e), sems[e], cnt)
        handles = {"pe": nc.tensor, "act": nc.scalar, "dve": nc.vector, "pool": nc.gpsimd, "sp": nc.sync}

        def run(e):
            h = handles[e]
            known = {}
            for o in self.prog[e]:
                need = {}
                for d in o.deps:
                    key, sem, val = d.tok
                    if known.get(key, 0) >= val:
                        continue
                    if key not in need or need[key][1] < val:
                        need[key] = (sem, val)
                for key, (sem, val) in need.items():
                    h.wait_ge(sem, val)
                    known[key] = val
                ins = o.fn(h)
                if o.tok is not None:
                    ins.then_inc(o.tok[1], 16 if o.is_dma else 1)
            q = self.dmas[e]
            if q:
                last = {}
                for o in q:
                    last[o.tok[0]] = (o.tok[1], o.tok[2])
                for key, (sem, val) in last.items():
                    if known.get(key, 0) < val:
                        h.wait_ge(sem, val)

        with nc.Block() as block:
            @block.sync
            def _(eng):
                run("sp")

            @block.scalar
            def _(eng):
                run("act")

            @block.vector
            def _(eng):
                run("dve")

            @block.gpsimd
            def _(eng):
                run("pool")

            @block.tensor
            def _(eng):
                run("pe")


def build_program(debug=()):
    nc = bass.Bass("TRN2", target_bir_lowering=False)
    S = Sched()

    def din(name, shape):
        return nc.dram_tensor(name, shape, F32, kind="ExternalInput").ap()

    def dout(name, shape):
        return nc.dram_tensor(name, shape, F32, kind="ExternalOutput").ap()

    xT_d = din("xT", [8, 128, NT])
    cst_d = din("cst", [128, NCST])
    gsub_d = din("gsub", [128, 512])
    ident_d = din("ident", [128, 128])
    perm_d = din("perm", [128, 128])
    ropeC_d = din("ropeC", [128, 1024])
    ropeS_d = din("ropeS", [128, 1024])
    kcT_d = din("kcT", [128, 4, 512])
    vc_d = din("vc", [128, 4, 512])
    wmod_d = din("wmod", [18, 128, 4096])
    up_d = [din("up1", [11, 128, 4096]), din("up2", [11, 128, 4096])]
    dn_d = [din("dn1", [4, 128, 5632]), din("dn2", [4, 128, 5632])]
    wq_d = din("wq", [128, 4096])
    wk_d = din("wk", [128, 4096])
    wv_d = din("wv", [128, 4096])
    wc_d = din("wc", [4, 128, 3072])
    wo_d = din("wo", [2, 128, 4096])
    yT_o = dout("yT", [8, 128, NT])
    kT_o = dout("kT", [4, 128, NT])
    v_o = dout("vout", [NT, 512])
    dbg_o = {}
    for name, shape in debug:
        dbg_o[name] = dout("dbg_" + name, shape)

    with ExitStack() as st:
        def sb(name, shape, dt):
            return st.enter_context(nc.sbuf_tensor(name, shape, dt))

        xT = sb("xTs", [128, 8, NT], F32)
        ring = sb("ring", [128, 3, 5632], BF16)
        R1 = sb("R1", [128, 5120], F32)
        R2 = sb("R2", [128, 14920], F32)
        R3 = sb("R3", [128, 5120], F32)
        rstd_t = sb("rstd_t", [128, 1280], F32)
        sq_t = sb("sq_t", [128, 2, 1280], BF16)
        tmp0_t = sb("tmp0_t", [128, 1280], F32)
        tmp1_t = sb("tmp1_t", [128, 1280], F32)
        cst = sb("csts", [128, NCST], F32)
        modT = sb("modT", [128, 3, 48], F32)
        acoef = sb("acoef", [128, 3, 16], F32)
        gcoef = sb("gcoef", [128, 3, 16], F32)
        ident = sb("idents", [128, 128], F32)
        perm = sb("perms", [128, 128], BF16)
        ones = sb("ones", [128, 128], BF16)
        gsub4 = sb("gsub4", [128, 4, 128], F32)
        silc = sb("silc", [128, 8, 2], BF16)
        small = sb("small", [128, 128], F32)
        ppt = sb("ppt", [128, 3, 512], F32)
        PT = sb("PT3", [128, 3, 512], BF16)
        ps = st.enter_context(nc.psum_tensor("ps", [128, 4096], F32))

        uT = R1[:].bitcast(BF16).rearrange("p (k t) -> p k t", k=8)
        hT = R2[:, 0:14080].bitcast(BF16).rearrange("p (k t) -> p k t", k=22)
        ycF = [R1[:].rearrange("p (k t) -> p k t", k=4)[:, c, :] for c in range(4)] + \
              [R3[:].rearrange("p (k t) -> p k t", k=4)[:, c, :] for c in range(4)]
        ycM = [R2[:, 0:10240].rearrange("p (k t) -> p k t", k=8)[:, c, :] for c in range(8)]
        mixT = R3[:].bitcast(BF16).rearrange("p (k t) -> p k t", k=8)
        kst = R3[:].rearrange("p (k t) -> p k t", k=4)
        QT = R2[:, 0:2560].bitcast(BF16).rearrange("p (h t) -> p h t", h=4)
        KT = R2[:, 2560:6144].bitcast(BF16).rearrange("p (h t) -> p h t", h=4)
        Vaug = R2[:, 6144:9784].bitcast(BF16).rearrange("p (t h e) -> p t h e", t=14, h=4)
        ropeC = R2[:, 10296:11320]
        ropeS = R2[:, 11320:12344]
        qbf = R2[:, 12344:12856].bitcast(BF16)
        Ocopy = R2[:, 12856:14920].rearrange("p (b m t e) -> p b m t e", b=2, m=2, t=4)
        QT1 = R2[:, 10296:12856].bitcast(BF16).rearrange("p (h t) -> p h t", h=4)
        rstd = rstd_t[:]
        Vo = R1[:, 0:2080].bitcast(BF16).rearrange("p (t h e) -> p t h e", t=8, h=4)

        class _Two:
            def __init__(self, ts):
                self.ts = ts

            def __getitem__(self, key):
                p, i, c = key
                return self.ts[i][p, c]
        sq = sq_t
        tmp = _Two([tmp0_t, tmp1_t])
        maxc = small[:, 0:28]
        mq = small[:, 28:29]
        mk = small[:, 29:30]
        negM = small[:, 30:31]
        biasAll = small[:, 32:69]
        lp = small[:, 70:72]
        le = small[:, 72:74]
        neglam = small[:, 74:75]
        nwf = small[:, 76:84].rearrange("p (a b) -> p a b", a=2)
        rs = small[:, 84:92].rearrange("p (m t) -> p m t", m=2)
        r1l = small[:, 92:96]
        ss4 = small[:, 96:100]
        rstd4 = small[:, 100:104]
        epsc = small[:, 104:105]
        sflag = small[:, 105:106]

        def cview(off, n):
            return cst[:, off:off + n]

        bmod2 = cview(C_BMOD, 144).rearrange("p (s x) -> p s x", s=3)
        gpre2 = cview(C_GPRE, 48).rearrange("p (s x) -> p s x", s=3)
        gpost2 = cview(C_GPOST, 48).rearrange("p (s x) -> p s x", s=3)
        convw = cview(C_CONVW, 12).rearrange("p (j i) -> p j i", j=4)
        cflag = cview(C_FLAG, 1)
        maskb = cview(C_MASK, 37)
        lamrow = cview(C_LAM, 256).rearrange("p (a b) -> p a b", a=4)
        condT = cview(C_COND, 16)

        PB = [Buf("pb%d" % i) for i in range(8)]
        RING = [Buf("ring%d" % i) for i in range(3)]
        XTa = [Buf("xTa%d" % i) for i in range(8)]
        XTb = [Buf("xTb%d" % i) for i in range(8)]
        XT = [Multi([XTa[i], XTb[i]]) for i in range(8)]
        UTK = [[Buf("uT%d_%d" % (i, j)) for j in range(3)] for i in range(8)]
        UT = [Multi(UTK[i]) for i in range(8)]
        HT = [Buf("hT%d" % i) for i in range(22)]
        YCF = [Buf("ycF%d" % i) for i in range(8)]
        YCM = [Buf("ycM%d" % i) for i in range(8)]
        MIX = [Buf("mix%d" % i) for i in range(8)]
        KST = [Buf("kst%d" % i) for i in range(4)]
        QTB = [Buf("QT%d" % i) for i in range(4)]
        KTB = [Buf("KT%d" % i) for i in range(4)]
        KTC = Buf("KTc")
        VA = [Buf("Vaug%d" % i) for i in range(14)]
        PTH = [[Buf("PT%d_%d" % (i, j)) for j in range(2)] for i in range(3)]
        PTB = [Multi(PTH[i]) for i in range(3)]
        ROPE = Buf("rope")
        OC = [Buf("Oc0"), Buf("Oc1")]
        QBF = Buf("qbf")
        QT1B = Buf("QT1")
        RSTD = Buf("rstd")
        SQ = [Buf("sq0"), Buf("sq1")]
        TMPK = [[Buf("tmp%d_%d" % (i, j)) for j in range(3)] for i in range(2)]
        TMP = [Multi(TMPK[i]) for i in range(2)]
        CST = Buf("cst")
        MODT = [Buf("modT%d" % i) for i in range(3)]
        COEF = [Buf("coef%d" % i) for i in range(3)]
        MODG = [Buf("modG%d" % i) for i in range(3)]
        GCOEF = [Buf("gcoef%d" % i) for i in range(3)]
        IDENT = Buf("ident")
        PERM = Buf("perm")
        ONES = Buf("ones")
        GSUB = Buf("gsub")
        SILC = Buf("silc")
        SM = Buf("small")
        SM0 = Buf("small0")
        BIASB = Buf("biasall")
        LAMB = Buf("lamb")
        SS4 = Buf("ss4")
        VOB = [Buf("Vo%d" % i) for i in range(8)]
        SFL = Buf("sflag")
        PP0, PP1, PP2 = Buf("pp0"), Buf("pp1"), Buf("pp2")
        NWF = Buf("nwf")
        R2_MIX = QTB + KTB + [KTC] + VA + [ROPE] + OC + [QBF, QT1B]

        def bank(b, n=512, off=0):
            return ps[:, 512 * b + off:512 * b + off + n]

        pieces = []
        for i in range(4):
            pieces.append((wmod_d[i], 4096))
        for j in range(11):
            pieces.append((up_d[0][j], 4096))
        for j in range(4):
            pieces.append((dn_d[0][j], 5632))
        for i in range(6, 12):
            pieces.append((wmod_d[i], 4096))
        pieces += [(wq_d, 4096), (wk_d, 4096), (wv_d, 4096)]
        for j in range(4):
            pieces.append((wc_d[j], 3072))
        pieces += [(wo_d[0], 4096), (wo_d[1], 4096)]
        for i in range(12, 18):
            pieces.append((wmod_d[i], 4096))
        for j in range(11):
            pieces.append((up_d[1][j], 4096))
        for j in range(4):
            pieces.append((dn_d[1][j], 5632))
        wstate = {"loaded": 0, "used": 0}

        def _load_piece(i):
            src, n = pieces[i]
            slot = i % 3
            extra = [XT[6]] if i in (4, 5) else []
            S.op("pool", lambda e, src=src, n=n, slot=slot: e.dma_start(out=ring[:, slot, 0:n], in_=src),
                 reads=extra, writes=[RING[slot]], dma=True)

        def prefetch(nahead):
            while wstate["loaded"] < min(len(pieces), wstate["used"] + nahead):
                _load_piece(wstate["loaded"])
                wstate["loaded"] += 1

        def next_piece(ncols_k):
            i = wstate["used"]
            wstate["used"] += 1
            while wstate["loaded"] < min(len(pieces), i + 3):
                _load_piece(wstate["loaded"])
                wstate["loaded"] += 1
            slot = i % 3
            n = pieces[i][1]
            return ring[:, slot, 0:n].rearrange("p (k c) -> p k c", k=ncols_k), RING[slot]

        S.op("sp", lambda e: e.dma_start(out=cst[:], in_=cst_d), writes=[CST], dma=True)
        S.op("sp", lambda e: e.dma_start(out=ident[:], in_=ident_d), writes=[IDENT], dma=True)
        S.op("sp", lambda e: e.dma_start(out=gsub4[:].rearrange("p a b -> p (a b)"), in_=gsub_d), writes=[GSUB], dma=True)
        for c in range(8):
            S.op("sp", lambda e, c=c: e.dma_start(out=xT[:, c, :], in_=xT_d[c]), writes=[XT[c]], dma=True)
        S.op("pool", lambda e: e.dma_start(out=perm[:], in_=perm_d), writes=[PERM], dma=True)
        S.op("pool", lambda e: e.memset(ones[:], 1.0), writes=[ONES])
        S.op("pool", lambda e: e.memset(epsc, EPS), writes=[SM0])
        S.op("act", lambda e: e.activation(out=silc[:].rearrange("p a b -> p (a b)"), in_=condT, func=AF.Silu),
             reads=[CST], writes=[SILC])

        def misc_setup():
            S.op("dve", lambda e: e.tensor_scalar(out=gsub4[:].rearrange("p a b -> p (a b)"), in0=gsub4[:].rearrange("p a b -> p (a b)"),
                                                  scalar1=1.0 - LAMBDA_INIT, scalar2=None, op0=ALU.mult),
                 reads=[GSUB], writes=[GSUB])
            S.op("dve", lambda e: e.tensor_tensor(out=ppt[:, 0, 0:128].rearrange("p (a b) -> p a b", a=2), in0=lamrow[:, 0:4:2, :],
                                                  in1=lamrow[:, 1:4:2, :], op=ALU.mult), reads=[CST], writes=[PP0])
            S.op("dve", lambda e: e.tensor_reduce(out=lp, in_=ppt[:, 0, 0:128].rearrange("p (a b) -> p a b", a=2), axis=AX.X, op=ALU.add),
                 reads=[PP0], writes=[SM])
            S.op("act", lambda e: e.activation(out=le, in_=lp, func=AF.Exp), reads=[SM], writes=[SM])
            S.op("dve", lambda e: e.tensor_tensor(out=neglam, in0=le[:, 1:2], in1=le[:, 0:1], op=ALU.subtract), reads=[SM], writes=[SM])
            S.op("dve", lambda e: e.tensor_scalar(out=neglam, in0=neglam, scalar1=-LAMBDA_INIT, scalar2=None, op0=ALU.add),
                 reads=[SM], writes=[SM, LAMB])
            S.op("dve", lambda e: e.tensor_scalar(out=nwf[:, 0, :], in0=convw[:, :, 0], scalar1=cflag, scalar2=-1.0, op0=ALU.mult, op1=ALU.mult),
                 reads=[CST], writes=[NWF])
            S.op("dve", lambda e: e.tensor_scalar(out=nwf[:, 1, :], in0=convw[:, :, 2], scalar1=cflag, scalar2=-1.0, op0=ALU.mult, op1=ALU.mult),
                 reads=[CST, NWF], writes=[NWF])

        def modulation(s, part, defer=False, hooks=None, wsrc=None):
            first = True
            for i in (range(0, 4) if part == "a" else range(4, 6)):
                if hooks is not None:
                    hooks[i]()
                if wsrc is not None:
                    w, wb = wsrc[i]
                else:
                    w, wb = next_piece(8)
                for cc in range(4):
                    ch = 4 * i + cc
                    for k in range(8):
                        S.op("pe", lambda e, w=w, cc=cc, k=k, ch=ch, f=first: e.matmul(
                            bank(7, 2, 2 * ch), lhsT=w[:, k, 128 * cc:128 * cc + 128], rhs=silc[:, k, :],
                            start=f, stop=(k == 7), skip_group_check=True),
                            reads=[wb, SILC], writes=[PB[7]])
                        first = False
            if part == "a":
                def evac_a():
                    S.op("dve", lambda e: e.tensor_tensor(out=modT[:, s, 0:32], in0=bank(7, 32), in1=bmod2[:, s, 0:32], op=ALU.add),
                         reads=[CST], excl=[PB[7]], writes=[MODT[s]])
                    S.op("dve", lambda e: e.scalar_tensor_tensor(out=acoef[:, s, :], in0=modT[:, s, 16:32], scalar=1.0, in1=gpre2[:, s, :],
                                                                 op0=ALU.add, op1=ALU.mult), reads=[MODT[s], CST], writes=[COEF[s]])
                if defer:
                    return evac_a
                evac_a()
            else:
                fac = 1.0 if s == 1 else 0.5

                def evac_b():
                    S.op("dve", lambda e: e.tensor_tensor(out=modT[:, s, 32:48], in0=bank(7, 16, 32), in1=bmod2[:, s, 32:48], op=ALU.add),
                         reads=[CST], excl=[PB[7]], writes=[MODG[s]])
                    S.op("dve", lambda e: e.scalar_tensor_tensor(out=gcoef[:, s, :], in0=modT[:, s, 32:48], scalar=fac, in1=gpost2[:, s, :],
                                                                 op0=ALU.mult, op1=ALU.mult), reads=[MODG[s], CST], writes=[GCOEF[s]])
                if defer:
                    return evac_b
                evac_b()

        def rstd_from_banks(b0, dim):
            for i, (a, b) in enumerate(BLK):
                S.op("act", lambda e, i=i, a=a, b=b: e.activation(out=rstd[:, a:b], in_=bank(b0 + i, b - a), func=AF.Ln,
                                                                  bias=epsc, scale=1.0 / dim),
                     reads=[SM0], excl=[PB[b0 + i]], writes=[RSTD])
            S.op("act", lambda e: e.activation(out=rstd, in_=rstd, func=AF.Exp, scale=-0.5), reads=[RSTD], writes=[RSTD])

        def stats_chunk(c, act_only=False):
            if c % 2 == 0 or act_only:
                S.op("act", lambda e, c=c: e.activation(out=sq[:, c % 2, :], in_=xT[:, c, :], func=AF.Square),
                     reads=[XT[c]], writes=[SQ[c % 2]])
            else:
                S.op("dve", lambda e, c=c: e.tensor_tensor(out=sq[:, c % 2, :], in0=xT[:, c, :], in1=xT[:, c, :], op=ALU.mult),
                     reads=[XT[c]], writes=[SQ[c % 2]])
            for i, (a, b) in enumerate(BLK):
                S.op("pe", lambda e, c=c, i=i, a=a, b=b: e.matmul(bank(i, b - a), lhsT=ones[:], rhs=sq[:, c % 2, a:b],
                                                                  start=(c == 0), stop=(c == 7)),
                     reads=[ONES, SQ[c % 2]], writes=[PB[i]])

        def prenorm_stats():
            for c in range(8):
                stats_chunk(c)
            rstd_from_banks(0, 1024.0)

        def prenorm_apply(s):
            for i, (a, b) in enumerate(BLK):
                g = 0 if i < 2 else 1
                for c in range(8):
                    S.op("dve", lambda e, c=c, a=a, b=b: e.tensor_tensor(out=tmp[:, c % 2, a:b], in0=xT[:, c, a:b], in1=rstd[:, a:b], op=ALU.mult),
                         reads=[XT[c], RSTD], writes=[TMPK[c % 2][i]])
                    S.op("act", lambda e, c=c, g=g, a=a, b=b: e.activation(
                        out=uT[:, c, a:b], in_=tmp[:, c % 2, a:b], func=AF.Identity,
                        bias=modT[:, s, 2 * c + g:2 * c + g + 1], scale=acoef[:, s, 2 * c + g:2 * c + g + 1]),
                        reads=[TMPK[c % 2][i], MODT[s], COEF[s]], writes=[UTK[c][i]])

        def post_stats(m):
            for i, (a, b) in enumerate(BLK):
                S.op("pe", lambda e, m=m, i=i, a=a, b=b: e.matmul(bank(3 + i, b - a), lhsT=ones[:], rhs=sq[:, m % 2, a:b],
                                                                  start=(m == 0), stop=(m == 7)),
                     reads=[ONES, SQ[m % 2]], writes=[PB[3 + i]])

        def out_proj(s, nk, wfn, rhs, RHS, yc, YC, next_stats=True, mod_next=None):
            for m in range(8):
                w, wb, col = wfn(m)
                for i, (a, b) in enumerate(BLK):
                    for k in range(nk):
                        S.op("pe", lambda e, w=w, col=col, i=i, a=a, b=b, k=k: e.matmul(
                            bank(i, b - a), lhsT=w[:, k, col:col + 128], rhs=rhs[k][:, a:b], start=(k == 0), stop=(k == nk - 1)),
                            reads=[wb, RHS[k]], writes=[PB[i]])
                    g = 0 if i < 2 else 1
                    S.op("act", lambda e, m=m, i=i, a=a, b=b: e.activation(out=sq[:, m % 2, a:b], in_=bank(i, b - a), func=AF.Square),
                         excl=[PB[i]], writes=[SQ[m % 2]])
                    S.op("act", lambda e, m=m, i=i, a=a, b=b, g=g: e.activation(out=yc[m][:, a:b], in_=bank(i, b - a), func=AF.Copy,
                                                                               scale=gcoef[:, s, 2 * m + g:2 * m + g + 1]),
                         reads=[GCOEF[s]], excl=[PB[i]], writes=[YC[m]])
                if m >= 1:
                    post_stats(m - 1)
            post_stats(7)
            prefetch(3)
            evs = mod_next() if mod_next is not None else None
            rstd_from_banks(3, 1024.0)
            for c in range(8):
                S.op("dve", lambda e, c=c: e.tensor_tensor(out=tmp[:, c % 2, :], in0=yc[c], in1=rstd, op=ALU.mult),
                     reads=[YC[c], RSTD], writes=[TMP[c % 2]])
                S.op("dve", lambda e, c=c: e.tensor_tensor(out=xT[:, c, :], in0=xT[:, c, :], in1=tmp[:, c % 2, :], op=ALU.add),
                     reads=[TMP[c % 2], XT[c]], writes=[XT[c]])
                if next_stats:
                    stats_chunk(c, act_only=True)
            if next_stats:
                rstd_from_banks(0, 1024.0)
            return evs

        def ffn(s, mid_hook=None, last=False, mod_next=None, pre_hook=None, extra_alias=()):
            for j in range(11):
                if j == 2 and pre_hook is not None:
                    pre_hook()
                if j == 6 and mid_hook is not None:
                    mid_hook()
                w, wb = next_piece(8)
                for mm in range(2):
                    m = 2 * j + mm
                    for i, (a, b) in enumerate(BLK):
                        for k in range(8):
                            S.op("pe", lambda e, w=w, mm=mm, i=i, a=a, b=b, k=k: e.matmul(
                                bank(i, b - a), lhsT=w[:, k, 128 * mm:128 * mm + 128], rhs=uT[:, k, a:b], start=(k == 0), stop=(k == 7)),
                                reads=[wb, UTK[k][i]], writes=[PB[i]])
                        for k in range(8):
                            S.op("pe", lambda e, w=w, mm=mm, i=i, a=a, b=b, k=k: e.matmul(
                                bank(3 + i, b - a), lhsT=w[:, k, 256 + 128 * mm:256 + 128 * mm + 128], rhs=uT[:, k, a:b],
                                start=(k == 0), stop=(k == 7)),
                                reads=[wb, UTK[k][i]], writes=[PB[3 + i]])
                        S.op("act", lambda e, m=m, i=i, a=a, b=b: e.activation(out=tmp[:, m % 2, a:b], in_=bank(i, b - a), func=AF.Silu),
                             excl=[PB[i]], writes=[TMP[m % 2]])
                        S.op("dve", lambda e, m=m, i=i, a=a, b=b: e.tensor_tensor(out=hT[:, m, a:b], in0=bank(3 + i, b - a),
                                                                                  in1=tmp[:, m % 2, a:b], op=ALU.mult),
                             reads=[TMP[m % 2]], excl=[PB[3 + i]], writes=[HT[m]])
            S.alias(YCF, UT + MIX + KST + list(extra_alias))
            dstate = {}

            def wfn(m):
                if m % 2 == 0:
                    dstate["w"] = next_piece(22)
                w, wb = dstate["w"]
                return w, wb, 128 * (m % 2)

            evs = out_proj(s, 22, wfn, [hT[:, k, :] for k in range(22)], HT, ycF, YCF, next_stats=not last, mod_next=mod_next)
            S.alias(UT + MIX + KST, YCF)
            return evs

        hooks0 = {0: lambda: None,
                  1: lambda: [stats_chunk(c) for c in (0, 1)],
                  2: lambda: [stats_chunk(c) for c in (2, 3)],
                  3: lambda: [stats_chunk(c) for c in (4, 5, 6, 7)]}
        ev_ = modulation(0, "a", defer=True, hooks=hooks0)
        rstd_from_banks(0, 1024.0)
        ev_()
        prenorm_apply(0)
        misc_setup()
        modb_w = R3[:, 0:4096].bitcast(BF16).rearrange("p (i k c) -> p i k c", i=2, k=8)
        MODBW = [Buf("modbw0"), Buf("modbw1")]
        S.alias(MODBW, MIX + KST)

        def load_modb():
            for ii in range(2):
                S.op("pool", lambda e, ii=ii: e.dma_start(out=modb_w[:, ii].rearrange("p k c -> p (k c)"), in_=wmod_d[4 + ii]),
                     writes=[MODBW[ii]], dma=True)
        modb_src = {4: (modb_w[:, 0], MODBW[0]), 5: (modb_w[:, 1], MODBW[1])}
        evs = ffn(0, mid_hook=lambda: modulation(0, "b", wsrc=modb_src), pre_hook=load_modb, extra_alias=MODBW,
                  mod_next=lambda: (modulation(1, "a", defer=True), modulation(1, "b", defer=True)))
        evs[0]()
        prenorm_apply(1)
        evs[1]()

        S.alias(R2_MIX, HT)
        S.op("sp", lambda e: e.dma_start(out=ropeC, in_=ropeC_d), writes=[ROPE], dma=True)
        S.op("sp", lambda e: e.dma_start(out=ropeS, in_=ropeS_d), writes=[ROPE], dma=True)
        S.op("pool", lambda e: e.dma_start(out=KT[:, :, 1280:1792], in_=kcT_d), writes=[KTC], dma=True)
        for i in range(4):
            S.op("pool", lambda e, i=i: e.dma_start(out=Vaug[:, 10 + i, :, 0:128], in_=vc_d[:, i, :].rearrange("p (h e) -> p h e", h=4)),
                 writes=[VA[10 + i]], dma=True)
        for t in range(14):
            S.op("dve", lambda e, t=t: e.memset(Vaug[:, t, :, 128:129], 1.0), writes=[VA[t]])

        def norm_max(src_ap, SRC, n, col):
            S.op("pe", lambda e: e.matmul(bank(7, n), lhsT=ones[:], rhs=src_ap, start=True, stop=True),
                 reads=[ONES, SRC], writes=[PB[7]])
            S.op("dve", lambda e: e.reduce_max(out=maxc[:, col:col + 1], in_=bank(7, n), axis=AX.X), excl=[PB[7]], writes=[SM])

        deferred = []

        def flush_deferred():
            while deferred:
                deferred.pop(0)()

        def qk_proj(dstT, DST, is_k, colbase):
            w, wb = next_piece(8)
            for i, (a, b) in enumerate(BLK):
                for h in range(4):
                    base = 0 if h % 2 == 0 else 3
                    pb = PB[base + i]
                    for k in range(8):
                        S.op("pe", lambda e, w=w, h=h, i=i, a=a, b=b, k=k, base=base: e.matmul(
                            bank(base + i, b - a), lhsT=w[:, k, 128 * h:128 * h + 128], rhs=uT[:, k, a:b], start=(k == 0), stop=(k == 7)),
                            reads=[wb, UTK[k][i]], writes=[pb])
                    flush_deferred()
                    src = bank(base + i, b - a)
                    sqi = (h * 3 + i) % 2
                    if i < 2:
                        S.op("act", lambda e, src=src, a=a, b=b: e.activation(out=qbf[:, a:b], in_=src, func=AF.Copy),
                             excl=[pb], writes=[QBF])
                    else:
                        S.op("act", lambda e, src=src, h=h, a=a, b=b: e.activation(out=dstT[:, h, a:b], in_=src, func=AF.Copy),
                             excl=[pb], writes=[DST[h]])
                    S.op("act", lambda e, src=src, sqi=sqi, a=a, b=b: e.activation(out=sq[:, sqi, 0:b - a], in_=src, func=AF.Square),
                         excl=[pb], writes=[SQ[sqi]])
                    if is_k:
                        S.op("act", lambda e, src=src, h=h, a=a, b=b: e.activation(out=kst[:, h, a:b], in_=src, func=AF.Copy),
                             excl=[pb], writes=[KST[h]])
                    if i < 2:
                        S.op("dve", lambda e, src=src, a=a, b=b: e.tensor_tensor(out=tmp[:, 0, a:b], in0=src, in1=ropeC[:, a:b], op=ALU.mult),
                             reads=[ROPE], excl=[pb], writes=[TMP[0]])

                        def rope_tail(h=h, a=a, b=b):
                            S.op("pe", lambda e: e.matmul(bank(6, 512), lhsT=perm[:], rhs=qbf[:, a:b], start=True, stop=True),
                                 reads=[PERM, QBF], writes=[PB[6]])
                            S.op("dve", lambda e: e.tensor_tensor(out=tmp[:, 1, a:b], in0=bank(6, 512), in1=ropeS[:, a:b], op=ALU.mult),
                                 reads=[ROPE], excl=[PB[6]], writes=[TMP[1]])
                            S.op("dve", lambda e: e.tensor_tensor(out=dstT[:, h, a:b], in0=tmp[:, 0, a:b], in1=tmp[:, 1, a:b], op=ALU.add),
                                 reads=[TMP[0], TMP[1]], writes=[DST[h]])
                        deferred.append(rope_tail)
                    deferred.append(lambda sqi=sqi, a=a, b=b, col=colbase + h * 3 + i: norm_max(sq[:, sqi, 0:b - a], SQ[sqi], b - a, col))
            if is_k:
                for h in range(4):
                    S.op("sp", lambda e, h=h: e.dma_start(out=kT_o[h], in_=kst[:, h, :]), reads=[KST[h]], dma=True)

        S.alias(KST, MIX)
        qk_proj(QT, QTB, False, 0)
        qk_proj(KT, KTB, True, 12)
        flush_deferred()
        S.alias([QT1B], [ROPE, QBF])
        S.op("pool", lambda e: e.memset(QT1[0:64, :, :], 0.0), writes=[QT1B])
        for h in range(4):
            S.op("pool", lambda e, h=h: e.tensor_copy(out=QT1[64:128, h, :], in_=QT[64:128, h, :]), reads=[QTB[h]], writes=[QT1B])
        for h in range(4):
            S.op("pool", lambda e, h=h: e.memset(QT[64:128, h, :], 0.0), reads=[QT1B], writes=[QTB[h]])
        for h in range(4):
            S.op("act", lambda e, h=h: e.activation(out=sq[:, h % 2, 0:512], in_=KT[:, h, 1280:1792], func=AF.Square),
                 reads=[KTC], writes=[SQ[h % 2]])
            norm_max(sq[:, h % 2, 0:512], SQ[h % 2], 512, 24 + h)
        S.op("dve", lambda e: e.reduce_max(out=mq, in_=maxc[:, 0:12], axis=AX.X), reads=[SM], writes=[SM])
        S.op("dve", lambda e: e.reduce_max(out=mk, in_=maxc[:, 12:28], axis=AX.X), reads=[SM], writes=[SM])
        S.op("dve", lambda e: e.tensor_tensor(out=negM, in0=mq, in1=mk, op=ALU.add), reads=[SM], writes=[SM])
        S.op("dve", lambda e: e.tensor_scalar(out=negM, in0=negM, scalar1=-0.5 * 0.125, scalar2=None, op0=ALU.mult), reads=[SM], writes=[SM])
        S.op("dve", lambda e: e.tensor_scalar(out=biasAll, in0=maskb, scalar1=negM, scalar2=None, op0=ALU.add), reads=[SM, CST], writes=[BIASB])

        w, wb = next_piece(8)
        for t in range(10):
            bk = t % 6
            for k in range(8):
                S.op("pe", lambda e, w=w, t=t, k=k, bk=bk: e.matmul(bank(bk, 512), lhsT=uT[:, k, 128 * t:128 * t + 128], rhs=w[:, k, :],
                                                                   start=(k == 0), stop=(k == 7)),
                     reads=[wb, UTK[k][min(t // 4, 2)]], writes=[PB[bk]])
            vs = tmp[:, t % 2, 0:512]
            S.op("act", lambda e, vs=vs, bk=bk: e.activation(out=vs, in_=bank(bk, 512), func=AF.Copy), excl=[PB[bk]], writes=[TMP[t % 2]])
            S.op("sp", lambda e, vs=vs, t=t: e.dma_start(out=v_o[128 * t:128 * t + 128, :], in_=vs), reads=[TMP[t % 2]], dma=True)
            S.op("dve", lambda e, vs=vs, t=t: e.tensor_copy(out=Vaug[:, t, :, 0:128], in_=vs.rearrange("p (h e) -> p h e", h=4)),
                 reads=[TMP[t % 2]], writes=[VA[t]])

        S.alias(MIX, KST)

        conv_state = {"next": 6, "open": False}

        def conv_gen():
            zb = tmp[:, 0, :]
            y = tmp[:, 1, :]
            bgS = sq_t[:].rearrange("p a b -> p (a b)").bitcast(F32)
            s0 = rstd
            s2 = ppt[:].rearrange("p a b -> p (a b)")[:, 0:NT]
            PPALL = Multi([PP0, PP1, PP2])
            CBANKS = [2, 3, 4, 5, 6]
            unit = 0
            for j in range(4):
                w, wb = next_piece(8)
                for typ in range(3):
                    for i, (a, b) in enumerate(BLK):
                        cb = CBANKS[unit % 5]
                        unit += 1
                        conv_state["open"] = True
                        for k in range(8):
                            S.op("pe", lambda e, w=w, typ=typ, cb=cb, a=a, b=b, k=k: e.matmul(
                                bank(cb, b - a), lhsT=w[:, k, 128 * typ:128 * typ + 128], rhs=uT[:, k, a:b], start=(k == 0), stop=(k == 7)),
                                reads=[wb, UTK[k][i]], writes=[PB[cb]])
                            if k < 7:
                                yield
                        if typ == 0:
                            S.op("act", lambda e, cb=cb, a=a, b=b: e.activation(out=zb[:, a:b], in_=bank(cb, b - a), func=AF.Copy),
                                 excl=[PB[cb]], writes=[TMP[0]])
                        elif typ == 1:
                            S.op("dve", lambda e, cb=cb, a=a, b=b: e.tensor_tensor(out=zb[:, a:b], in0=bank(cb, b - a), in1=zb[:, a:b], op=ALU.mult),
                                 reads=[TMP[0]], excl=[PB[cb]], writes=[TMP[0]])
                        else:
                            S.op("act", lambda e, cb=cb, a=a, b=b: e.activation(out=bgS[:, a:b], in_=bank(cb, b - a), func=AF.Copy),
                                 excl=[PB[cb]], writes=[SQ[0], SQ[1]])
                        conv_state["open"] = False
                        yield
                    if typ == 1:
                        z = zb
                        S.op("act", lambda e, j=j: e.activation(out=y, in_=z, func=AF.Copy, scale=convw[:, j, 1:2]),
                             reads=[TMP[0], CST], writes=[TMP[1]])
                        S.op("act", lambda e, j=j: e.activation(out=s0, in_=z, func=AF.Copy, scale=convw[:, j, 0:1]),
                             reads=[TMP[0], CST], writes=[RSTD])
                        S.op("act", lambda e, j=j: e.activation(out=s2, in_=z, func=AF.Copy, scale=convw[:, j, 2:3]),
                             reads=[TMP[0], CST], writes=[PPALL])
                        for (lo, hi) in GRP:
                            S.op("dve", lambda e, lo=lo, hi=hi: e.tensor_tensor(out=y[:, lo + 1:hi], in0=y[:, lo + 1:hi], in1=s0[:, lo:hi - 1], op=ALU.add),
                                 reads=[RSTD, TMP[1]], writes=[TMP[1]])
                            S.op("dve", lambda e, lo=lo, hi=hi: e.tensor_tensor(out=y[:, lo:hi - 1], in0=y[:, lo:hi - 1], in1=s2[:, lo + 1:hi], op=ALU.add),
                                 reads=[PPALL, TMP[1]], writes=[TMP[1]])
                        S.op("dve", lambda e, j=j: e.scalar_tensor_tensor(out=y[:, 256:1024:256], in0=z[:, 255:1023:256], scalar=nwf[:, 0, j:j + 1],
                                                                          in1=y[:, 256:1024:256], op0=ALU.mult, op1=ALU.add),
                             reads=[TMP[0], NWF, TMP[1]], writes=[TMP[1]])
                        S.op("dve", lambda e, j=j: e.scalar_tensor_tensor(out=y[:, 255:1023:256], in0=z[:, 256:1024:256], scalar=nwf[:, 1, j:j + 1],
                                                                          in1=y[:, 255:1023:256], op0=ALU.mult, op1=ALU.add),
                             reads=[TMP[0], NWF, TMP[1]], writes=[TMP[1]])
                S.op("dve", lambda e, j=j: e.tensor_tensor(out=mixT[:, 4 + j, :], in0=bgS, in1=y, op=ALU.mult),
                     reads=[TMP[1], SQ[0], SQ[1]], writes=[MIX[4 + j]])

        o_t = ppt[:, 0, :].rearrange("p (t e) -> p t e", t=4)
        t1_t = ppt[:, 1, :].rearrange("p (t e) -> p t e", t=4)
        on_t = ppt[:, 2, :].rearrange("p (t e) -> p t e", t=4)
        osq_t = t1_t
        pp_state = {"n": 0, "ob": 0}

        pending = []

        def postproc(ob, nt, h, tok0, n):
            O = Ocopy[:, ob]
            OB = OC[ob]
            pp_state["n"] += 1
            S.op("dve", lambda e: e.reciprocal(out=rs[:, :, 0:nt], in_=O[:, :, 0:nt, 128]), reads=[OB], writes=[SM])
            S.op("dve", lambda e: e.tensor_scalar(out=r1l[:, 0:nt], in0=rs[:, 1, 0:nt], scalar1=neglam, scalar2=None, op0=ALU.mult),
                 reads=[SM, LAMB], writes=[SM])
            S.op("dve", lambda e: e.tensor_tensor(out=o_t[:, 0:nt, :], in0=O[:, 0, 0:nt, 0:128],
                                                  in1=rs[:, 0, 0:nt].unsqueeze(2).to_broadcast([128, nt, 128]), op=ALU.mult),
                 reads=[OB, SM], writes=[PP0])
            S.op("dve", lambda e: e.tensor_tensor(out=t1_t[:, 0:nt, :], in0=O[:, 1, 0:nt, 0:128],
                                                  in1=r1l[:, 0:nt].unsqueeze(2).to_broadcast([128, nt, 128]), op=ALU.mult),
                 reads=[OB, SM], writes=[PP1])
            S.op("dve", lambda e: e.tensor_tensor(out=o_t[:, 0:nt, :], in0=o_t[:, 0:nt, :], in1=t1_t[:, 0:nt, :], op=ALU.add),
                 reads=[PP0, PP1], writes=[PP0])
            S.op("dve", lambda e: e.tensor_tensor(out=osq_t[:, 0:nt, :], in0=o_t[:, 0:nt, :], in1=o_t[:, 0:nt, :], op=ALU.mult),
                 reads=[PP0], writes=[PP1])
            S.op("dve", lambda e: e.tensor_reduce(out=ss4[:, 0:nt], in_=osq_t[:, 0:nt, :], axis=AX.X, op=ALU.add),
                 reads=[PP1], writes=[SS4])

            def stage2():
                S.op("act", lambda e: e.activation(out=ss4[:, 0:nt], in_=ss4[:, 0:nt], func=AF.Ln, bias=epsc, scale=1.0 / 128.0),
                     reads=[SS4, SM0], writes=[SS4])
                S.op("act", lambda e: e.activation(out=rstd4[:, 0:nt], in_=ss4[:, 0:nt], func=AF.Exp, scale=-0.5),
                     reads=[SS4], writes=[SS4])
                S.op("dve", lambda e: e.tensor_tensor(out=on_t[:, 0:nt, :], in0=o_t[:, 0:nt, :],
                                                      in1=rstd4[:, 0:nt].unsqueeze(2).to_broadcast([128, nt, 128]), op=ALU.mult),
                     reads=[PP0, SS4], writes=[PP2])
                S.op("dve", lambda e: e.tensor_tensor(out=on_t[:, 0:nt, :], in0=on_t[:, 0:nt, :], in1=gsub4[:, 0:nt, :], op=ALU.mult),
                     reads=[PP2, GSUB], writes=[PP2])

            def stage3():
                while conv_state["open"]:
                    next(cg_, None)
                tb = 6
                for t in range(nt):
                    S.op("pe", lambda e, t=t: e.transpose(out=bank(tb, 128, 128 * t), in_=on_t[:, t, :], identity=ident[:]),
                         reads=[PP2, IDENT], writes=[PB[tb]])
                S.op("dve", lambda e: e.tensor_copy(out=mixT[:, h, tok0:tok0 + 128 * nt], in_=bank(tb, 128 * nt)),
                     excl=[PB[tb]], writes=[MIX[h]])
            d2, d3 = (8, 12) if (nt == 4 and n + 13 < 192) else (3, 6)
            pending.append((n + d2, stage2))
            pending.append((n + d3, stage3))

        def run_pending(n):
            while pending and pending[0][0] <= n:
                pending.pop(0)[1]()

        SBANKS = [0, 1, 7]
        iters = []
        for h in range(4):
            for qb in range(2):
                for m in range(2):
                    for ci in range(12):
                        if ci < 8:
                            chunk = (128 * ci, ci, [36])
                        else:
                            chunk = (1280 + 128 * (ci - 8), 10 + ci - 8, [36])
                        iters.append((h, m, 512 * qb, 512, 4, ci, 12, chunk))
        for h in range(4):
            for m in range(2):
                for ci in range(2):
                    iters.append((h, m, 1024, 256, 2, ci, 2, (1024 + 128 * ci, 8 + ci, [36])))

        def emit_S(n):
            h, m, q0, nq, nt, ci, nch, (kc0, vt, bcols) = iters[n]
            sb_ = SBANKS[n % 3]
            Qm = QT if m == 0 else QT1
            S.op("pe", lambda e: e.matmul(bank(sb_, nq), lhsT=KT[:, h, kc0:kc0 + 128],
                                          rhs=Qm[:, h, q0:q0 + nq], start=True, stop=True),
                 reads=[KTB[h], KTC, QTB[h], QT1B], writes=[PB[sb_]])

        def emit_exp_pv(n):
            h, m, q0, nq, nt, ci, nch, (kc0, vt, bcols) = iters[n]
            sb_ = SBANKS[n % 3]
            pt_ = n % 3
            ob0 = 2 if m == 0 else 4
            seg = nq // len(bcols)
            for si, bc in enumerate(bcols):
                S.op("act", lambda e, si=si, bc=bc: e.activation(
                    out=PT[:, pt_, si * seg:(si + 1) * seg], in_=bank(sb_, seg, si * seg), func=AF.Exp,
                    bias=biasAll[:, bc:bc + 1], scale=0.125),
                    reads=[BIASB], excl=[PB[sb_]], writes=[PTH[pt_][si] if len(bcols) == 2 else PTB[pt_]])
            for t in range(nt):
                bb = t // 2
                if nt == 4 and vt < 8 and (q0 + 128 * t) // 256 != vt // 2:
                    vsrc, VB = Vo[:, vt, h, 0:129], VOB[vt]
                else:
                    vsrc, VB = Vaug[:, vt, h, 0:129], VA[vt]
                S.op("pe", lambda e, t=t, bb=bb, vsrc=vsrc: e.matmul(
                    bank(ob0 + bb, 129, 129 * (t % 2)), lhsT=PT[:, pt_, 128 * t:128 * t + 128], rhs=vsrc,
                    start=(ci == 0 and t % 2 == 0), stop=(ci == nch - 1), skip_group_check=True),
                    reads=[PTB[pt_], VB], writes=[PB[ob0 + bb]])
            if ci == nch - 1:
                ob = pp_state["ob"]
                nbank = (nt + 1) // 2
                for bb in range(nbank):
                    ntb = min(2, nt - 2 * bb)
                    S.op("dve", lambda e, bb=bb, ntb=ntb: e.tensor_copy(
                        out=Ocopy[:, ob, m, 2 * bb:2 * bb + ntb, :], in_=bank(ob0 + bb, 129 * ntb).rearrange("p (t e) -> p t e", t=ntb)),
                        excl=[PB[ob0 + bb]], writes=[OC[ob]])
                if m == 1:
                    postproc(ob, nt, h, q0, n)
                    pp_state["ob"] = 1 - ob

        prefetch(2)
        cg_ = conv_gen()
        for _ in cg_:
            pass
        S.alias(VOB, UT)
        S.op("dve", lambda e: e.tensor_scalar(out=sflag, in0=cflag, scalar1=-1.0, scalar2=1.0, op0=ALU.mult, op1=ALU.add),
             reads=[CST], writes=[SFL])
        for t in range(8):
            S.op("act", lambda e, t=t: e.activation(out=Vo[:, t, :, :].rearrange("p h e -> p (h e)"),
                                                     in_=Vaug[:, t, :, :].rearrange("p h e -> p (h e)"),
                                                     func=AF.Copy, scale=sflag),
                 reads=[VA[t], SFL], writes=[VOB[t]])
        for t in range(10, 14):
            S.op("act", lambda e, t=t: e.activation(out=Vaug[:, t, :, :].rearrange("p h e -> p (h e)"),
                                                     in_=Vaug[:, t, :, :].rearrange("p h e -> p (h e)"),
                                                     func=AF.Copy, scale=sflag),
                 reads=[VA[t], SFL], writes=[VA[t]])
        emit_S(0)
        emit_S(1)
        for n in range(len(iters)):
            if n + 2 < len(iters):
                emit_S(n + 2)
            emit_exp_pv(n)
            run_pending(n)
        run_pending(10 ** 9)
        for _ in cg_:
            pass

        S.alias(UT, VOB)
        S.alias(YCM, R2_MIX)
        wostate = {}

        def wofn(m):
            if m % 4 == 0:
                wostate["w"] = next_piece(8)
            w, wb = wostate["w"]
            return w, wb, 128 * (m % 4)

        evs = out_proj(1, 8, wofn, [mixT[:, k, :] for k in range(8)], MIX, ycM, YCM,
                       mod_next=lambda: (modulation(2, "a", defer=True), modulation(2, "b", defer=True)))
        S.alias(HT, YCM + R2_MIX)
        evs[0]()
        prenorm_apply(2)
        evs[1]()
        ffn(2, last=True)

        for c in range(8):
            S.op("sp", lambda e, c=c: e.dma_start(out=yT_o[c], in_=xT[:, c, :]), reads=[XT[c]], dma=True)

        S.emit(nc, st)
    return nc


def _rope_tables():
    t = np.arange(1024)
    row = (t // 64).astype(np.float32)
    col = (t % 64).astype(np.float32)
    half = 32
    freqs = (10000.0 ** (-np.arange(0, half, 2, dtype=np.float32) / half)).astype(np.float32)
    C = np.zeros((128, 1024), np.float32)
    Sg = np.zeros((128, 1024), np.float32)
    P = np.zeros((128, 128), np.float32)
    for p in range(128):
        d = p % 64
        pos = row if d < 32 else col
        dd = d % 32
        f = freqs[dd % 16]
        ang = (pos * f).astype(np.float32)
        C[p] = np.cos(ang)
        if dd < 16:
            Sg[p] = -np.sin(ang)
            partner = p + 16
        else:
            Sg[p] = np.sin(ang)
            partner = p - 16
        P[partner, p] = 1.0
    return C, Sg, P


def _pieces_cols(W, col_lists):
    K = W.shape[0] // 128
    out = []
    for cols in col_lists:
        sub = W[:, cols]
        sub = sub.reshape(K, 128, len(cols)).transpose(1, 0, 2)
        out.append(np.ascontiguousarray(sub).reshape(128, K * len(cols)))
    return np.stack(out, 0)


_PROG = {}


def kernel(x_prompt, x_sample, c, cache_k, cache_v, c_ctx, w_mod, b_mod, norm_pre, norm_post,
           ffn1_up, ffn1_down, ffn2_up, ffn2_down, w_in, conv_w, lam_qk, subln_g, w_o):
    f = lambda a: np.ascontiguousarray(np.asarray(a, dtype=np.float32))
    x_prompt, x_sample, c, cache_k, cache_v, c_ctx = map(f, (x_prompt, x_sample, c, cache_k, cache_v, c_ctx))
    w_mod, b_mod, norm_pre, norm_post = f(w_mod)[0], f(b_mod)[0], f(norm_pre)[0], f(norm_post)[0]
    ffn1_up, ffn1_down, ffn2_up, ffn2_down = f(ffn1_up)[0], f(ffn1_down)[0], f(ffn2_up)[0], f(ffn2_down)[0]
    w_in, conv_w, lam_qk, subln_g, w_o = f(w_in)[0], f(conv_w)[0], f(lam_qk)[0], f(subln_g)[0], f(w_o)[0]

    ar = np.arange
    wmodP = _pieces_cols(w_mod, [ar(512 * i, 512 * i + 512) for i in range(18)])
    upcols = [np.concatenate([ar(256 * j, 256 * j + 256), ar(2816 + 256 * j, 2816 + 256 * j + 256)]) for j in range(11)]
    up1P = _pieces_cols(ffn1_up, upcols)
    up2P = _pieces_cols(ffn2_up, upcols)
    dn1P = _pieces_cols(ffn1_down, [ar(256 * j, 256 * j + 256) for j in range(4)])
    dn2P = _pieces_cols(ffn2_down, [ar(256 * j, 256 * j + 256) for j in range(4)])
    wqP = _pieces_cols(w_in, [ar(0, 512)])[0]
    wkP = _pieces_cols(w_in, [ar(512, 1024)])[0]
    wvP = _pieces_cols(w_in, [ar(1024, 1536)])[0]
    wcP = _pieces_cols(w_in, [np.concatenate([ar(2560 + 128 * j, 2560 + 128 * j + 128), ar(2048 + 128 * j, 2048 + 128 * j + 128),
                                              ar(1536 + 128 * j, 1536 + 128 * j + 128)]) for j in range(4)])
    woP = _pieces_cols(w_o, [ar(0, 512), ar(512, 1024)])

    ropeC, ropeS, perm = _rope_tables()
    ident = np.eye(128, dtype=np.float32)
    gsub = np.ascontiguousarray(np.broadcast_to(np.tile(subln_g, 4)[None, :], (128, 512))).astype(np.float32)

    def tmaj(v):
        return np.ascontiguousarray(v.reshape(-1, 128).T)

    shared_cst = np.zeros((128, NCST), np.float32)
    bm = tmaj(b_mod)
    shared_cst[:, C_BMOD:C_BMOD + 144] = np.repeat(bm, 2, axis=1)
    gp = np.stack([tmaj(norm_pre[s]) for s in range(3)], 1)
    shared_cst[:, C_GPRE:C_GPRE + 48] = np.repeat(gp.reshape(128, 24), 2, axis=1)
    gq = np.stack([tmaj(norm_post[s]) for s in range(3)], 1)
    shared_cst[:, C_GPOST:C_GPOST + 48] = np.repeat(gq.reshape(128, 24), 2, axis=1)
    cw = np.stack([tmaj(conv_w[i]) for i in range(3)], 2)
    shared_cst[:, C_CONVW:C_CONVW + 12] = cw.reshape(128, 12)
    shared_cst[:, C_LAM:C_LAM + 256] = lam_qk.reshape(1, 256)

    in_maps = []
    groups = []
    for core in range(8):
        if core < 6:
            tokA = x_prompt[5 * core:5 * core + 4].reshape(1024, 1024)
            tokB = x_prompt[5 * core + 4]
            condA = c_ctx
            sample = None
        else:
            sample = core - 6
            tokA = x_sample[sample]
            tokB = x_prompt[30 + sample]
            condA = c[sample]
        tok = np.concatenate([tokA, tokB], 0)
        xT = np.ascontiguousarray(tok.T).reshape(8, 128, NT)
        cst = shared_cst.copy()
        cond2 = np.stack([tmaj(condA), tmaj(c_ctx)], 2)
        cst[:, C_COND:C_COND + 16] = cond2.reshape(128, 16)
        mask = np.zeros(37, np.float32)
        if sample is None:
            for cc in range(8):
                for j in range(4):
                    mask[cc * 4 + j] = 0.0 if (cc // 2) == j else NEG
            mask[32:36] = NEG
            cst[:, C_FLAG] = 1.0
            rc, rs_ = np.ones_like(ropeC), np.zeros_like(ropeS)
            cb = 0
        else:
            rc, rs_ = ropeC, ropeS
            cb = sample
        cst[:, C_MASK:C_MASK + 37] = mask[None, :]
        ck = cache_k[cb, 0]
        kcT = np.ascontiguousarray(ck.transpose(2, 1, 0))
        cv = cache_v[cb, 0].reshape(4, 128, 512)
        vc = np.ascontiguousarray(cv.transpose(1, 0, 2))
        in_maps.append({
            "xT": xT, "cst": cst, "gsub": gsub, "ident": ident, "perm": perm, "ropeC": rc, "ropeS": rs_,
            "kcT": kcT, "vc": vc, "wmod": wmodP, "up1": up1P, "up2": up2P, "dn1": dn1P, "dn2": dn2P,
            "wq": wqP, "wk": wkP, "wv": wvP, "wc": wcP, "wo": woP,
        })

    if "nc" not in _PROG:
        _PROG["nc"] = build_program()
    res = run_bass_kernel_spmd(_PROG["nc"], in_maps, core_ids=list(range(8)))

    y_prompt = np.zeros((32, 256, 1024), np.float32)
    y_sample = np.zeros((2, 1024, 1024), np.float32)
    new_k = np.zeros((32, 1, 256, 4, 128), np.float32)
    new_v = np.zeros((32, 1, 256, 4, 128), np.float32)
    for core in range(8):
        r = res.results[core]
        y = r["yT"].reshape(1024, NT).T
        k = r["kT"].reshape(4, 128, NT).transpose(2, 0, 1)
        v = r["vout"].reshape(NT, 4, 128)
        if core < 6:
            for i in range(4):
                y_prompt[5 * core + i] = y[256 * i:256 * i + 256]
                new_k[5 * core + i, 0] = k[256 * i:256 * i + 256]
                new_v[5 * core + i, 0] = v[256 * i:256 * i + 256]
            bB = 5 * core + 4
        else:
            y_sample[core - 6] = y[0:1024]
            bB = 30 + core - 6
        y_prompt[bB] = y[1024:1280]
        new_k[bB, 0] = k[1024:1280]
        new_v[bB, 0] = v[1024:1280]
    return (y_prompt, y_sample, new_k, new_v)
```

```python
import math
import numpy as np
import concourse.bass as bass
import concourse.mybir as mybir
from concourse.bass_utils import run_bass_kernel_spmd
from contextlib import ExitStack

F32 = mybir.dt.float32
BF16 = mybir.dt.bfloat16
AF = mybir.ActivationFunctionType
ALU = mybir.AluOpType
AX = mybir.AxisListType

DMA_K = 12
NT = 1280
BLK = [(0, 512), (512, 1024), (1024, 1280)]
GRP = [(0, 1024), (1024, 1280)]
EPS = 1e-6
NEG = -30000.0
SPL = 1024
LAMBDA_INIT = 0.2
C_BMOD = 0
C_GPRE = 144
C_GPOST = 192
C_CONVW = 240
C_FLAG = 252
C_MASK = 253
C_LAM = 290
C_COND = 546
NCST = 562


class Buf:
    __slots__ = ("name", "lw", "rd")

    def __init__(self, name):
        self.name = name
        self.lw = None
        self.rd = {}


class Multi(list):
    pass


def _flat(bufs):
    out = []
    for b in bufs:
        if isinstance(b, Multi):
            out.extend(b)
        else:
            out.append(b)
    return out


class Op:
    __slots__ = ("eng", "fn", "deps", "needed", "tok", "is_dma", "qidx")

    def __init__(self, eng, fn, is_dma):
        self.eng = eng
        self.fn = fn
        self.deps = []
        self.needed = False
        self.tok = None
        self.is_dma = is_dma
        self.qidx = -1


class Sched:
    ENGS = ("pe", "act", "dve", "pool", "sp")

    def __init__(self):
        self.prog = {e: [] for e in self.ENGS}
        self.dmas = {e: [] for e in self.ENGS}

    def _dep(self, op, prod, raw, cross_only=False):
        if prod is None or prod is op:
            return
        if (not prod.is_dma) and prod.eng == op.eng and not op.is_dma:
            if op.eng == "pe" or cross_only:
                return
            if not raw:
                return
        prod.needed = True
        op.deps.append(prod)

    def op(self, eng, fn, reads=(), writes=(), dma=False, excl=()):
        o = Op(eng, fn, dma)
        reads = _flat(reads)
        writes = _flat(writes)
        for b in excl:
            self._dep(o, b.lw, False, cross_only=True)
            for r in b.rd.values():
                self._dep(o, r, False, cross_only=True)
        for b in reads:
            self._dep(o, b.lw, True)
        for b in writes:
            self._dep(o, b.lw, False)
            for r in b.rd.values():
                self._dep(o, r, False)
        if dma:
            q = self.dmas[eng]
            o.qidx = len(q)
            if o.qidx >= DMA_K:
                prev = q[o.qidx - DMA_K]
                prev.needed = True
                o.deps.append(prev)
            q.append(o)
            o.needed = True
        for b in reads:
            key = id(o) if dma else eng
            b.rd[key] = o
        for b in writes:
            b.lw = o
            b.rd = {}
        for b in excl:
            b.lw = o
            b.rd = {}
        self.prog[eng].append(o)
        return o

    def alias(self, new_bufs, old_bufs):
        new_bufs = _flat(new_bufs)
        old_bufs = _flat(old_bufs)
        users = {}
        for ob in old_bufs:
            if ob.lw is not None:
                users[id(ob.lw)] = ob.lw
            for r in ob.rd.values():
                users[id(r)] = r
        for nb in new_bufs:
            if nb.lw is not None:
                users[id(nb.lw)] = nb.lw
            for r in nb.rd.values():
                users[id(r)] = r
        for nb in new_bufs:
            nb.lw = None
            nb.rd = dict(users)

    def emit(self, nc, stack):
        sems = {}
        for e in self.ENGS:
            sems[e] = stack.enter_context(nc.semaphore("s_" + e))
        dsems = {}
        for e in self.ENGS:
            if self.dmas[e]:
                dsems[e] = [stack.enter_context(nc.semaphore("d_%s%d" % (e, i))) for i in range(DMA_K)]
        for e in self.ENGS:
            cnt = 0
            for o in self.prog[e]:
                if o.is_dma:
                    o.tok = (("d", e, o.qidx % DMA_K), dsems[e][o.qidx % DMA_K], 16 * (o.qidx // DMA_K + 1))
                elif o.needed:
                    cnt += 1
                    o.tok = (("c", e), sems[e], cnt)
        handles = {"pe": nc.tensor, "act": nc.scalar, "dve": nc.vector, "pool": nc.gpsimd, "sp": nc.sync}

        def run(e):
            h = handles[e]
            known = {}
            for o in self.prog[e]:
                need = {}
                for d in o.deps:
                    key, sem, val = d.tok
                    if known.get(key, 0) >= val:
                        continue
                    if key not in need or need[key][1] < val:
                        need[key] = (sem, val)
                for key, (sem, val) in need.items():
                    h.wait_ge(sem, val)
                    known[key] = val
                ins = o.fn(h)
                if o.tok is not None:
                    ins.then_inc(o.tok[1], 16 if o.is_dma else 1)
            q = self.dmas[e]
            if q:
                last = {}
                for o in q:
                    last[o.tok[0]] = (o.tok[1], o.tok[2])
                for key, (sem, val) in last.items():
                    if known.get(key, 0) < val:
                        h.wait_ge(sem, val)

        with nc.Block() as block:
            @block.sync
            def _(eng):
                run("sp")

            @block.scalar
            def _(eng):
                run("act")

            @block.vector
            def _(eng):
                run("dve")

            @block.gpsimd
            def _(eng):
                run("pool")

            @block.tensor
            def _(eng):
                run("pe")


def build_program(debug=()):
    nc = bass.Bass("TRN2", target_bir_lowering=False)
    S = Sched()

    def din(name, shape):
        return nc.dram_tensor(name, shape, F32, kind="ExternalInput").ap()

    def dout(name, shape):
        return nc.dram_tensor(name, shape, F32, kind="ExternalOutput").ap()

    xT_d = din("xT", [8, 128, NT])
    cst_d = din("cst", [128, NCST])
    gsub_d = din("gsub", [128, 512])
    ident_d = din("ident", [128, 128])
    perm_d = din("perm", [128, 128])
    ropeC_d = din("ropeC", [128, 1024])
    ropeS_d = din("ropeS", [128, 1024])
    kcT_d = din("kcT", [128, 4, 512])
    vc_d = din("vc", [128, 4, 512])
    wmod_d = din("wmod", [18, 128, 4096])
    up_d = [din("up1", [11, 128, 4096]), din("up2", [11, 128, 4096])]
    dn_d = [din("dn1", [4, 128, 5632]), din("dn2", [4, 128, 5632])]
    wq_d = din("wq", [128, 4096])
    wk_d = din("wk", [128, 4096])
    wv_d = din("wv", [128, 4096])
    wc_d = din("wc", [4, 128, 3072])
    wo_d = din("wo", [2, 128, 4096])
    yT_o = dout("yT", [8, 128, NT])
    kT_o = dout("kT", [4, 128, NT])
    v_o = dout("vout", [NT, 512])
    dbg_o = {}
    for name, shape in debug:
        dbg_o[name] = dout("dbg_" + name, shape)

    with ExitStack() as st:
        def sb(name, shape, dt):
            return st.enter_context(nc.sbuf_tensor(name, shape, dt))

        xT = sb("xTs", [128, 8, NT], F32)
        ring = sb("ring", [128, 3, 5632], BF16)
        R1 = sb("R1", [128, 5120], F32)
        R2 = sb("R2", [128, 14920], F32)
        R3 = sb("R3", [128, 5120], F32)
        rstd_t = sb("rstd_t", [128, 1280], F32)
        sq_t = sb("sq_t", [128, 2, 1280], BF16)
        tmp0_t = sb("tmp0_t", [128, 1280], F32)
        tmp1_t = sb("tmp1_t", [128, 1280], F32)
        cst = sb("csts", [128, NCST], F32)
        modT = sb("modT", [128, 3, 48], F32)
        acoef = sb("acoef", [128, 3, 16], F32)
        gcoef = sb("gcoef", [128, 3, 16], F32)
        ident = sb("idents", [128, 128], F32)
        perm = sb("perms", [128, 128], BF16)
        ones = sb("ones", [128, 128], BF16)
        gsub4 = sb("gsub4", [128, 4, 128], F32)
        silc = sb("silc", [128, 8, 2], BF16)
        small = sb("small", [128, 128], F32)
        ppt = sb("ppt", [128, 3, 512], F32)
        PT = sb("PT3", [128, 3, 512], BF16)
        ps = st.enter_context(nc.psum_tensor("ps", [128, 4096], F32))

        uT = R1[:].bitcast(BF16).rearrange("p (k t) -> p k t", k=8)
        hT = R2[:, 0:14080].bitcast(BF16).rearrange("p (k t) -> p k t", k=22)
        ycF = [R1[:].rearrange("p (k t) -> p k t", k=4)[:, c, :] for c in range(4)] + \
              [R3[:].rearrange("p (k t) -> p k t", k=4)[:, c, :] for c in range(4)]
        ycM = [R2[:, 0:10240].rearrange("p (k t) -> p k t", k=8)[:, c, :] for c in range(8)]
        mixT = R3[:].bitcast(BF16).rearrange("p (k t) -> p k t", k=8)
        kst = R3[:].rearrange("p (k t) -> p k t", k=4)
        QT = R2[:, 0:2560].bitcast(BF16).rearrange("p (h t) -> p h t", h=4)
        KT = R2[:, 2560:6144].bitcast(BF16).rearrange("p (h t) -> p h t", h=4)
        Vaug = R2[:, 6144:9784].bitcast(BF16).rearrange("p (t h e) -> p t h e", t=14, h=4)
        ropeC = R2[:, 10296:11320]
        ropeS = R2[:, 11320:12344]
        qbf = R2[:, 12344:12856].bitcast(BF16)
        Ocopy = R2[:, 12856:14920].rearrange("p (b m t e) -> p b m t e", b=2, m=2, t=4)
        QT1 = R2[:, 10296:12856].bitcast(BF16).rearrange("p (h t) -> p h t", h=4)
        rstd = rstd_t[:]
        Vo = R1[:, 0:2080].bitcast(BF16).rearrange("p (t h e) -> p t h e", t=8, h=4)

        class _Two:
            def __init__(self, ts):
                self.ts = ts

            def __getitem__(self, key):
                p, i, c = key
                return self.ts[i][p, c]
        sq = sq_t
        tmp = _Two([tmp0_t, tmp1_t])
        maxc = small[:, 0:28]
        mq = small[:, 28:29]
        mk = small[:, 29:30]
        negM = small[:, 30:31]
        biasAll = small[:, 32:69]
        lp = small[:, 70:72]
        le = small[:, 72:74]
        neglam = small[:, 74:75]
        nwf = small[:, 76:84].rearrange("p (a b) -> p a b", a=2)
        rs = small[:, 84:92].rearrange("p (m t) -> p m t", m=2)
        r1l = small[:, 92:96]
        ss4 = small[:, 96:100]
        rstd4 = small[:, 100:104]
        epsc = small[:, 104:105]
        sflag = small[:, 105:106]

        def cview(off, n):
            return cst[:, off:off + n]

        bmod2 = cview(C_BMOD, 144).rearrange("p (s x) -> p s x", s=3)
        gpre2 = cview(C_GPRE, 48).rearrange("p (s x) -> p s x", s=3)
        gpost2 = cview(C_GPOST, 48).rearrange("p (s x) -> p s x", s=3)
        convw = cview(C_CONVW, 12).rearrange("p (j i) -> p j i", j=4)
        cflag = cview(C_FLAG, 1)
        maskb = cview(C_MASK, 37)
        lamrow = cview(C_LAM, 256).rearrange("p (a b) -> p a b", a=4)
        condT = cview(C_COND, 16)

        PB = [Buf("pb%d" % i) for i in range(8)]
        RING = [Buf("ring%d" % i) for i in range(3)]
        XTa = [Buf("xTa%d" % i) for i in range(8)]
        XTb = [Buf("xTb%d" % i) for i in range(8)]
        XT = [Multi([XTa[i], XTb[i]]) for i in range(8)]
        UTK = [[Buf("uT%d_%d" % (i, j)) for j in range(3)] for i in range(8)]
        UT = [Multi(UTK[i]) for i in range(8)]
        HT = [Buf("hT%d" % i) for i in range(22)]
        YCF = [Buf("ycF%d" % i) for i in range(8)]
        YCM = [Buf("ycM%d" % i) for i in range(8)]
        MIX = [Buf("mix%d" % i) for i in range(8)]
        KST = [Buf("kst%d" % i) for i in range(4)]
        QTB = [Buf("QT%d" % i) for i in range(4)]
        KTB = [Buf("KT%d" % i) for i in range(4)]
        KTC = Buf("KTc")
        VA = [Buf("Vaug%d" % i) for i in range(14)]
        PTH = [[Buf("PT%d_%d" % (i, j)) for j in range(2)] for i in range(3)]
        PTB = [Multi(PTH[i]) for i in range(3)]
        ROPE = Buf("rope")
        OC = [Buf("Oc0"), Buf("Oc1")]
        QBF = Buf("qbf")
        QT1B = Buf("QT1")
        RSTD = Buf("rstd")
        SQ = [Buf("sq0"), Buf("sq1")]
        TMPK = [[Buf("tmp%d_%d" % (i, j)) for j in range(3)] for i in range(2)]
        TMP = [Multi(TMPK[i]) for i in range(2)]
        CST = Buf("cst")
        MODT = [Buf("modT%d" % i) for i in range(3)]
        COEF = [Buf("coef%d" % i) for i in range(3)]
        MODG = [Buf("modG%d" % i) for i in range(3)]
        GCOEF = [Buf("gcoef%d" % i) for i in range(3)]
        IDENT = Buf("ident")
        PERM = Buf("perm")
        ONES = Buf("ones")
        GSUB = Buf("gsub")
        SILC = Buf("silc")
        SM = Buf("small")
        SM0 = Buf("small0")
        BIASB = Buf("biasall")
        LAMB = Buf("lamb")
        SS4 = Buf("ss4")
        VOB = [Buf("Vo%d" % i) for i in range(8)]
        SFL = Buf("sflag")
        PP0, PP1, PP2 = Buf("pp0"), Buf("pp1"), Buf("pp2")
        NWF = Buf("nwf")
        R2_MIX = QTB + KTB + [KTC] + VA + [ROPE] + OC + [QBF, QT1B]

        def bank(b, n=512, off=0):
            return ps[:, 512 * b + off:512 * b + off + n]

        pieces = []
        for i in range(4):
            pieces.append((wmod_d[i], 4096))
        for j in range(11):
            pieces.append((up_d[0][j], 4096))
        for j in range(4):
            pieces.append((dn_d[0][j], 5632))
        for i in range(6, 12):
            pieces.append((wmod_d[i], 4096))
        pieces += [(wq_d, 4096), (wk_d, 4096), (wv_d, 4096)]
        for j in range(4):
            pieces.append((wc_d[j], 3072))
        pieces += [(wo_d[0], 4096), (wo_d[1], 4096)]
        for i in range(12, 18):
            pieces.append((wmod_d[i], 4096))
        for j in range(11):
            pieces.append((up_d[1][j], 4096))
        for j in range(4):
            pieces.append((dn_d[1][j], 5632))
        wstate = {"loaded": 0, "used": 0}

        def _load_piece(i):
            src, n = pieces[i]
            slot = i % 3
            extra = [XT[6]] if i in (4, 5) else []
            S.op("pool", lambda e, src=src, n=n, slot=slot: e.dma_start(out=ring[:, slot, 0:n], in_=src),
                 reads=extra, writes=[RING[slot]], dma=True)

        def prefetch(nahead):
            while wstate["loaded"] < min(len(pieces), wstate["used"] + nahead):
                _load_piece(wstate["loaded"])
                wstate["loaded"] += 1

        def next_piece(ncols_k):
            i = wstate["used"]
            wstate["used"] += 1
            while wstate["loaded"] < min(len(pieces), i + 3):
                _load_piece(wstate["loaded"])
                wstate["loaded"] += 1
            slot = i % 3
            n = pieces[i][1]
            return ring[:, slot, 0:n].rearrange("p (k c) -> p k c", k=ncols_k), RING[slot]

        S.op("sp", lambda e: e.dma_start(out=cst[:], in_=cst_d), writes=[CST], dma=True)
        S.op("sp", lambda e: e.dma_start(out=ident[:], in_=ident_d), writes=[IDENT], dma=True)
        S.op("sp", lambda e: e.dma_start(out=gsub4[:].rearrange("p a b -> p (a b)"), in_=gsub_d), writes=[GSUB], dma=True)
        for c in range(8):
            S.op("sp", lambda e, c=c: e.dma_start(out=xT[:, c, :], in_=xT_d[c]), writes=[XT[c]], dma=True)
        S.op("pool", lambda e: e.dma_start(out=perm[:], in_=perm_d), writes=[PERM], dma=True)
        S.op("pool", lambda e: e.memset(ones[:], 1.0), writes=[ONES])
        S.op("pool", lambda e: e.memset(epsc, EPS), writes=[SM0])
        S.op("act", lambda e: e.activation(out=silc[:].rearrange("p a b -> p (a b)"), in_=condT, func=AF.Silu),
             reads=[CST], writes=[SILC])

        def misc_setup():
            S.op("dve", lambda e: e.tensor_scalar(out=gsub4[:].rearrange("p a b -> p (a b)"), in0=gsub4[:].rearrange("p a b -> p (a b)"),
                                                  scalar1=1.0 - LAMBDA_INIT, scalar2=None, op0=ALU.mult),
                 reads=[GSUB], writes=[GSUB])
            S.op("dve", lambda e: e.tensor_tensor(out=ppt[:, 0, 0:128].rearrange("p (a b) -> p a b", a=2), in0=lamrow[:, 0:4:2, :],
                                                  in1=lamrow[:, 1:4:2, :], op=ALU.mult), reads=[CST], writes=[PP0])
            S.op("dve", lambda e: e.tensor_reduce(out=lp, in_=ppt[:, 0, 0:128].rearrange("p (a b) -> p a b", a=2), axis=AX.X, op=ALU.add),
                 reads=[PP0], writes=[SM])
            S.op("act", lambda e: e.activation(out=le, in_=lp, func=AF.Exp), reads=[SM], writes=[SM])
            S.op("dve", lambda e: e.tensor_tensor(out=neglam, in0=le[:, 1:2], in1=le[:, 0:1], op=ALU.subtract), reads=[SM], writes=[SM])
            S.op("dve", lambda e: e.tensor_scalar(out=neglam, in0=neglam, scalar1=-LAMBDA_INIT, scalar2=None, op0=ALU.add),
                 reads=[SM], writes=[SM, LAMB])
            S.op("dve", lambda e: e.tensor_scalar(out=nwf[:, 0, :], in0=convw[:, :, 0], scalar1=cflag, scalar2=-1.0, op0=ALU.mult, op1=ALU.mult),
                 reads=[CST], writes=[NWF])
            S.op("dve", lambda e: e.tensor_scalar(out=nwf[:, 1, :], in0=convw[:, :, 2], scalar1=cflag, scalar2=-1.0, op0=ALU.mult, op1=ALU.mult),
                 reads=[CST, NWF], writes=[NWF])

        def modulation(s, part, defer=False, hooks=None, wsrc=None):
            first = True
            for i in (range(0, 4) if part == "a" else range(4, 6)):
                if hooks is not None:
                    hooks[i]()
                if wsrc is not None:
                    w, wb = wsrc[i]
                else:
                    w, wb = next_piece(8)
                for cc in range(4):
                    ch = 4 * i + cc
                    for k in range(8):
                        S.op("pe", lambda e, w=w, cc=cc, k=k, ch=ch, f=first: e.matmul(
                            bank(7, 2, 2 * ch), lhsT=w[:, k, 128 * cc:128 * cc + 128], rhs=silc[:, k, :],
                            start=f, stop=(k == 7), skip_group_check=True),
                            reads=[wb, SILC], writes=[PB[7]])
                        first = False
            if part == "a":
                def evac_a():
                    S.op("dve", lambda e: e.tensor_tensor(out=modT[:, s, 0:32], in0=bank(7, 32), in1=bmod2[:, s, 0:32], op=ALU.add),
                         reads=[CST], excl=[PB[7]], writes=[MODT[s]])
                    S.op("dve", lambda e: e.scalar_tensor_tensor(out=acoef[:, s, :], in0=modT[:, s, 16:32], scalar=1.0, in1=gpre2[:, s, :],
                                                                 op0=ALU.add, op1=ALU.mult), reads=[MODT[s], CST], writes=[COEF[s]])
                if defer:
                    return evac_a
                evac_a()
            else:
                fac = 1.0 if s == 1 else 0.5

                def evac_b():
                    S.op("dve", lambda e: e.tensor_tensor(out=modT[:, s, 32:48], in0=bank(7, 16, 32), in1=bmod2[:, s, 32:48], op=ALU.add),
                         reads=[CST], excl=[PB[7]], writes=[MODG[s]])
                    S.op("dve", lambda e: e.scalar_tensor_tensor(out=gcoef[:, s, :], in0=modT[:, s, 32:48], scalar=fac, in1=gpost2[:, s, :],
                                                                 op0=ALU.mult, op1=ALU.mult), reads=[MODG[s], CST], writes=[GCOEF[s]])
                if defer:
                    return evac_b
                evac_b()

        def rstd_from_banks(b0, dim):
            for i, (a, b) in enumerate(BLK):
                S.op("act", lambda e, i=i, a=a, b=b: e.activation(out=rstd[:, a:b], in_=bank(b0 + i, b - a), func=AF.Ln,
                                                                  bias=epsc, scale=1.0 / dim),
                     reads=[SM0], excl=[PB[b0 + i]], writes=[RSTD])
            S.op("act", lambda e: e.activation(out=rstd, in_=rstd, func=AF.Exp, scale=-0.5), reads=[RSTD], writes=[RSTD])

        def stats_chunk(c, act_only=False):
            if c % 2 == 0 or act_only:
                S.op("act", lambda e, c=c: e.activation(out=sq[:, c % 2, :], in_=xT[:, c, :], func=AF.Square),
                     reads=[XT[c]], writes=[SQ[c % 2]])
            else:
                S.op("dve", lambda e, c=c: e.tensor_tensor(out=sq[:, c % 2, :], in0=xT[:, c, :], in1=xT[:, c, :], op=ALU.mult),
                     reads=[XT[c]], writes=[SQ[c % 2]])
            for i, (a, b) in enumerate(BLK):
                S.op("pe", lambda e, c=c, i=i, a=a, b=b: e.matmul(bank(i, b - a), lhsT=ones[:], rhs=sq[:, c % 2, a:b],
                                                                  start=(c == 0), stop=(c == 7)),
                     reads=[ONES, SQ[c % 2]], writes=[PB[i]])

        def prenorm_stats():
            for c in range(8):
                stats_chunk(c)
            rstd_from_banks(0, 1024.0)

        def prenorm_apply(s):
            for i, (a, b) in enumerate(BLK):
                g = 0 if i < 2 else 1
                for c in range(8):
                    S.op("dve", lambda e, c=c, a=a, b=b: e.tensor_tensor(out=tmp[:, c % 2, a:b], in0=xT[:, c, a:b], in1=rstd[:, a:b], op=ALU.mult),
                         reads=[XT[c], RSTD], writes=[TMPK[c % 2][i]])
                    S.op("act", lambda e, c=c, g=g, a=a, b=b: e.activation(
                        out=uT[:, c, a:b], in_=tmp[:, c % 2, a:b], func=AF.Identity,
                        bias=modT[:, s, 2 * c + g:2 * c + g + 1], scale=acoef[:, s, 2 * c + g:2 * c + g + 1]),
                        reads=[TMPK[c % 2][i], MODT[s], COEF[s]], writes=[UTK[c][i]])

        def post_stats(m):
            for i, (a, b) in enumerate(BLK):
                S.op("pe", lambda e, m=m, i=i, a=a, b=b: e.matmul(bank(3 + i, b - a), lhsT=ones[:], rhs=sq[:, m % 2, a:b],
                                                                  start=(m == 0), stop=(m == 7)),
                     reads=[ONES, SQ[m % 2]], writes=[PB[3 + i]])

        def out_proj(s, nk, wfn, rhs, RHS, yc, YC, next_stats=True, mod_next=None):
            for m in range(8):
                w, wb, col = wfn(m)
                for i, (a, b) in enumerate(BLK):
                    for k in range(nk):
                        S.op("pe", lambda e, w=w, col=col, i=i, a=a, b=b, k=k: e.matmul(
                            bank(i, b - a), lhsT=w[:, k, col:col + 128], rhs=rhs[k][:, a:b], start=(k == 0), stop=(k == nk - 1)),
                            reads=[wb, RHS[k]], writes=[PB[i]])
                    g = 0 if i < 2 else 1
                    S.op("act", lambda e, m=m, i=i, a=a, b=b: e.activation(out=sq[:, m % 2, a:b], in_=bank(i, b - a), func=AF.Square),
                         excl=[PB[i]], writes=[SQ[m % 2]])
                    S.op("act", lambda e, m=m, i=i, a=a, b=b, g=g: e.activation(out=yc[m][:, a:b], in_=bank(i, b - a), func=AF.Copy,
                                                                               scale=gcoef[:, s, 2 * m + g:2 * m + g + 1]),
                         reads=[GCOEF[s]], excl=[PB[i]], writes=[YC[m]])
                if m >= 1:
                    post_stats(m - 1)
            post_stats(7)
            prefetch(3)
            evs = mod_next() if mod_next is not None else None
            rstd_from_banks(3, 1024.0)
            for c in range(8):
                S.op("dve", lambda e, c=c: e.tensor_tensor(out=tmp[:, c % 2, :], in0=yc[c], in1=rstd, op=ALU.mult),
                     reads=[YC[c], RSTD], writes=[TMP[c % 2]])
                S.op("dve", lambda e, c=c: e.tensor_tensor(out=xT[:, c, :], in0=xT[:, c, :], in1=tmp[:, c % 2, :], op=ALU.add),
                     reads=[TMP[c % 2], XT[c]], writes=[XT[c]])
                if next_stats:
                    stats_chunk(c, act_only=True)
            if next_stats:
                rstd_from_banks(0, 1024.0)
            return evs

        def ffn(s, mid_hook=None, last=False, mod_next=None, pre_hook=None, extra_alias=()):
            for j in range(11):
                if j == 2 and pre_hook is not None:
                    pre_hook()
                if j == 6 and mid_hook is not None:
                    mid_hook()
                w, wb = next_piece(8)
                for mm in range(2):
                    m = 2 * j + mm
                    for i, (a, b) in enumerate(BLK):
                        for k in range(8):
                            S.op("pe", lambda e, w=w, mm=mm, i=i, a=a, b=b, k=k: e.matmul(
                                bank(i, b - a), lhsT=w[:, k, 128 * mm:128 * mm + 128], rhs=uT[:, k, a:b], start=(k == 0), stop=(k == 7)),
                                reads=[wb, UTK[k][i]], writes=[PB[i]])
                        for k in range(8):
                            S.op("pe", lambda e, w=w, mm=mm, i=i, a=a, b=b, k=k: e.matmul(
                                bank(3 + i, b - a), lhsT=w[:, k, 256 + 128 * mm:256 + 128 * mm + 128], rhs=uT[:, k, a:b],
                                start=(k == 0), stop=(k == 7)),
                                reads=[wb, UTK[k][i]], writes=[PB[3 + i]])
                        S.op("act", lambda e, m=m, i=i, a=a, b=b: e.activation(out=tmp[:, m % 2, a:b], in_=bank(i, b - a), func=AF.Silu),
                             excl=[PB[i]], writes=[TMP[m % 2]])
                        S.op("dve", lambda e, m=m, i=i, a=a, b=b: e.tensor_tensor(out=hT[:, m, a:b], in0=bank(3 + i, b - a),
                                                                                  in1=tmp[:, m % 2, a:b], op=ALU.mult),
                             reads=[TMP[m % 2]], excl=[PB[3 + i]], writes=[HT[m]])
            S.alias(YCF, UT + MIX + KST + list(extra_alias))
            dstate = {}

            def wfn(m):
                if m % 2 == 0:
                    dstate["w"] = next_piece(22)
                w, wb = dstate["w"]
                return w, wb, 128 * (m % 2)

            evs = out_proj(s, 22, wfn, [hT[:, k, :] for k in range(22)], HT, ycF, YCF, next_stats=not last, mod_next=mod_next)
            S.alias(UT + MIX + KST, YCF)
            return evs

        hooks0 = {0: lambda: None,
                  1: lambda: [stats_chunk(c) for c in (0, 1)],
                  2: lambda: [stats_chunk(c) for c in (2, 3)],
                  3: lambda: [stats_chunk(c) for c in (4, 5, 6, 7)]}
        ev_ = modulation(0, "a", defer=True, hooks=hooks0)
        rstd_from_banks(0, 1024.0)
        ev_()
        prenorm_apply(0)
        misc_setup()
        modb_w = R3[:, 0:4096].bitcast(BF16).rearrange("p (i k c) -> p i k c", i=2, k=8)
        MODBW = [Buf("modbw0"), Buf("modbw1")]
        S.alias(MODBW, MIX + KST)

        def load_modb():
            for ii in range(2):
                S.op("pool", lambda e, ii=ii: e.dma_start(out=modb_w[:, ii].rearrange("p k c -> p (k c)"), in_=wmod_d[4 + ii]),
                     writes=[MODBW[ii]], dma=True)
        modb_src = {4: (modb_w[:, 0], MODBW[0]), 5: (modb_w[:, 1], MODBW[1])}
        evs = ffn(0, mid_hook=lambda: modulation(0, "b", wsrc=modb_src), pre_hook=load_modb, extra_alias=MODBW,
                  mod_next=lambda: (modulation(1, "a", defer=True), modulation(1, "b", defer=True)))
        evs[0]()
        prenorm_apply(1)
        evs[1]()

        S.alias(R2_MIX, HT)
        S.op("sp", lambda e: e.dma_start(out=ropeC, in_=ropeC_d), writes=[ROPE], dma=True)
        S.op("sp", lambda e: e.dma_start(out=ropeS, in_=ropeS_d), writes=[ROPE], dma=True)
        S.op("pool", lambda e: e.dma_start(out=KT[:, :, 1280:1792], in_=kcT_d), writes=[KTC], dma=True)
        for i in range(4):
            S.op("pool", lambda e, i=i: e.dma_start(out=Vaug[:, 10 + i, :, 0:128], in_=vc_d[:, i, :].rearrange("p (h e) -> p h e", h=4)),
                 writes=[VA[10 + i]], dma=True)
        for t in range(14):
            S.op("dve", lambda e, t=t: e.memset(Vaug[:, t, :, 128:129], 1.0), writes=[VA[t]])

        def norm_max(src_ap, SRC, n, col):
            S.op("pe", lambda e: e.matmul(bank(7, n), lhsT=ones[:], rhs=src_ap, start=True, stop=True),
                 reads=[ONES, SRC], writes=[PB[7]])
            S.op("dve", lambda e: e.reduce_max(out=maxc[:, col:col + 1], in_=bank(7, n), axis=AX.X), excl=[PB[7]], writes=[SM])

        deferred = []

        def flush_deferred():
            while deferred:
                deferred.pop(0)()

        def qk_proj(dstT, DST, is_k, colbase):
            w, wb = next_piece(8)
            for i, (a, b) in enumerate(BLK):
                for h in range(4):
                    base = 0 if h % 2 == 0 else 3
                    pb = PB[base + i]
                    for k in range(8):
                        S.op("pe", lambda e, w=w, h=h, i=i, a=a, b=b, k=k, base=base: e.matmul(
                            bank(base + i, b - a), lhsT=w[:, k, 128 * h:128 * h + 128], rhs=uT[:, k, a:b], start=(k == 0), stop=(k == 7)),
                            reads=[wb, UTK[k][i]], writes=[pb])
                    flush_deferred()
                    src = bank(base + i, b - a)
                    sqi = (h * 3 + i) % 2
                    if i < 2:
                        S.op("act", lambda e, src=src, a=a, b=b: e.activation(out=qbf[:, a:b], in_=src, func=AF.Copy),
                             excl=[pb], writes=[QBF])
                    else:
                        S.op("act", lambda e, src=src, h=h, a=a, b=b: e.activation(out=dstT[:, h, a:b], in_=src, func=AF.Copy),
                             excl=[pb], writes=[DST[h]])
                    S.op("act", lambda e, src=src, sqi=sqi, a=a, b=b: e.activation(out=sq[:, sqi, 0:b - a], in_=src, func=AF.Square),
                         excl=[pb], writes=[SQ[sqi]])
                    if is_k:
                        S.op("act", lambda e, src=src, h=h, a=a, b=b: e.activation(out=kst[:, h, a:b], in_=src, func=AF.Copy),
                             excl=[pb], writes=[KST[h]])
                    if i < 2:
                        S.op("dve", lambda e, src=src, a=a, b=b: e.tensor_tensor(out=tmp[:, 0, a:b], in0=src, in1=ropeC[:, a:b], op=ALU.mult),
                             reads=[ROPE], excl=[pb], writes=[TMP[0]])

                        def rope_tail(h=h, a=a, b=b):
                            S.op("pe", lambda e: e.matmul(bank(6, 512), lhsT=perm[:], rhs=qbf[:, a:b], start=True, stop=True),
                                 reads=[PERM, QBF], writes=[PB[6]])
                            S.op("dve", lambda e: e.tensor_tensor(out=tmp[:, 1, a:b], in0=bank(6, 512), in1=ropeS[:, a:b], op=ALU.mult),
                                 reads=[ROPE], excl=[PB[6]], writes=[TMP[1]])
                            S.op("dve", lambda e: e.tensor_tensor(out=dstT[:, h, a:b], in0=tmp[:, 0, a:b], in1=tmp[:, 1, a:b], op=ALU.add),
                                 reads=[TMP[0], TMP[1]], writes=[DST[h]])
                        deferred.append(rope_tail)
                    deferred.append(lambda sqi=sqi, a=a, b=b, col=colbase + h * 3 + i: norm_max(sq[:, sqi, 0:b - a], SQ[sqi], b - a, col))
            if is_k:
                for h in range(4):
                    S.op("sp", lambda e, h=h: e.dma_start(out=kT_o[h], in_=kst[:, h, :]), reads=[KST[h]], dma=True)

        S.alias(KST, MIX)
        qk_proj(QT, QTB, False, 0)
        qk_proj(KT, KTB, True, 12)
        flush_deferred()
        S.alias([QT1B], [ROPE, QBF])
        S.op("pool", lambda e: e.memset(QT1[0:64, :, :], 0.0), writes=[QT1B])
        for h in range(4):
            S.op("pool", lambda e, h=h: e.tensor_copy(out=QT1[64:128, h, :], in_=QT[64:128, h, :]), reads=[QTB[h]], writes=[QT1B])
        for h in range(4):
            S.op("pool", lambda e, h=h: e.memset(QT[64:128, h, :], 0.0), reads=[QT1B], writes=[QTB[h]])
        for h in range(4):
            S.op("act", lambda e, h=h: e.activation(out=sq[:, h % 2, 0:512], in_=KT[:, h, 1280:1792], func=AF.Square),
                 reads=[KTC], writes=[SQ[h % 2]])
            norm_max(sq[:, h % 2, 0:512], SQ[h % 2], 512, 24 + h)
        S.op("dve", lambda e: e.reduce_max(out=mq, in_=maxc[:, 0:12], axis=AX.X), reads=[SM], writes=[SM])
        S.op("dve", lambda e: e.reduce_max(out=mk, in_=maxc[:, 12:28], axis=AX.X), reads=[SM], writes=[SM])
        S.op("dve", lambda e: e.tensor_tensor(out=negM, in0=mq, in1=mk, op=ALU.add), reads=[SM], writes=[SM])
        S.op("dve", lambda e: e.tensor_scalar(out=negM, in0=negM, scalar1=-0.5 * 0.125, scalar2=None, op0=ALU.mult), reads=[SM], writes=[SM])
        S.op("dve", lambda e: e.tensor_scalar(out=biasAll, in0=maskb, scalar1=negM, scalar2=None, op0=ALU.add), reads=[SM, CST], writes=[BIASB])

        w, wb = next_piece(8)
        for t in range(10):
            bk = t % 6
            for k in range(8):
                S.op("pe", lambda e, w=w, t=t, k=k, bk=bk: e.matmul(bank(bk, 512), lhsT=uT[:, k, 128 * t:128 * t + 128], rhs=w[:, k, :],
                                                                   start=(k == 0), stop=(k == 7)),
                     reads=[wb, UTK[k][min(t // 4, 2)]], writes=[PB[bk]])
            vs = tmp[:, t % 2, 0:512]
            S.op("act", lambda e, vs=vs, bk=bk: e.activation(out=vs, in_=bank(bk, 512), func=AF.Copy), excl=[PB[bk]], writes=[TMP[t % 2]])
            S.op("sp", lambda e, vs=vs, t=t: e.dma_start(out=v_o[128 * t:128 * t + 128, :], in_=vs), reads=[TMP[t % 2]], dma=True)
            S.op("dve", lambda e, vs=vs, t=t: e.tensor_copy(out=Vaug[:, t, :, 0:128], in_=vs.rearrange("p (h e) -> p h e", h=4)),
                 reads=[TMP[t % 2]], writes=[VA[t]])

        S.alias(MIX, KST)

        conv_state = {"next": 6, "open": False}

        def conv_gen():
            zb = tmp[:, 0, :]
            y = tmp[:, 1, :]
            bgS = sq_t[:].rearrange("p a b -> p (a b)").bitcast(F32)
            s0 = rstd
            s2 = ppt[:].rearrange("p a b -> p (a b)")[:, 0:NT]
            PPALL = Multi([PP0, PP1, PP2])
            CBANKS = [2, 3, 4, 5, 6]
            unit = 0
            for j in range(4):
                w, wb = next_piece(8)
                for typ in range(3):
                    for i, (a, b) in enumerate(BLK):
                        cb = CBANKS[unit % 5]
                        unit += 1
                        conv_state["open"] = True
                        for k in range(8):
                            S.op("pe", lambda e, w=w, typ=typ, cb=cb, a=a, b=b, k=k: e.matmul(
                                bank(cb, b - a), lhsT=w[:, k, 128 * typ:128 * typ + 128], rhs=uT[:, k, a:b], start=(k == 0), stop=(k == 7)),
                                reads=[wb, UTK[k][i]], writes=[PB[cb]])
                            if k < 7:
                                yield
                        if typ == 0:
                            S.op("act", lambda e, cb=cb, a=a, b=b: e.activation(out=zb[:, a:b], in_=bank(cb, b - a), func=AF.Copy),
                                 excl=[PB[cb]], writes=[TMP[0]])
                        elif typ == 1:
                            S.op("dve", lambda e, cb=cb, a=a, b=b: e.tensor_tensor(out=zb[:, a:b], in0=bank(cb, b - a), in1=zb[:, a:b], op=ALU.mult),
                                 reads=[TMP[0]], excl=[PB[cb]], writes=[TMP[0]])
                        else:
                            S.op("act", lambda e, cb=cb, a=a, b=b: e.activation(out=bgS[:, a:b], in_=bank(cb, b - a), func=AF.Copy),
                                 excl=[PB[cb]], writes=[SQ[0], SQ[1]])
                        conv_state["open"] = False
                        yield
                    if typ == 1:
                        z = zb
                        S.op("act", lambda e, j=j: e.activation(out=y, in_=z, func=AF.Copy, scale=convw[:, j, 1:2]),
                             reads=[TMP[0], CST], writes=[TMP[1]])
                        S.op("act", lambda e, j=j: e.activation(out=s0, in_=z, func=AF.Copy, scale=convw[:, j, 0:1]),
                             reads=[TMP[0], CST], writes=[RSTD])
                        S.op("act", lambda e, j=j: e.activation(out=s2, in_=z, func=AF.Copy, scale=convw[:, j, 2:3]),
                             reads=[TMP[0], CST], writes=[PPALL])
                        for (lo, hi) in GRP:
                            S.op("dve", lambda e, lo=lo, hi=hi: e.tensor_tensor(out=y[:, lo + 1:hi], in0=y[:, lo + 1:hi], in1=s0[:, lo:hi - 1], op=ALU.add),
                                 reads=[RSTD, TMP[1]], writes=[TMP[1]])
                            S.op("dve", lambda e, lo=lo, hi=hi: e.tensor_tensor(out=y[:, lo:hi - 1], in0=y[:, lo:hi - 1], in1=s2[:, lo + 1:hi], op=ALU.add),
                                 reads=[PPALL, TMP[1]], writes=[TMP[1]])
                        S.op("dve", lambda e, j=j: e.scalar_tensor_tensor(out=y[:, 256:1024:256], in0=z[:, 255:1023:256], scalar=nwf[:, 0, j:j + 1],
                                                                          in1=y[:, 256:1024:256], op0=ALU.mult, op1=ALU.add),
                             reads=[TMP[0], NWF, TMP[1]], writes=[TMP[1]])
                        S.op("dve", lambda e, j=j: e.scalar_tensor_tensor(out=y[:, 255:1023:256], in0=z[:, 256:1024:256], scalar=nwf[:, 1, j:j + 1],
                                                                          in1=y[:, 255:1023:256], op0=ALU.mult, op1=ALU.add),
                             reads=[TMP[0], NWF, TMP[1]], writes=[TMP[1]])
                S.op("dve", lambda e, j=j: e.tensor_tensor(out=mixT[:, 4 + j, :], in0=bgS, in1=y, op=ALU.mult),
                     reads=[TMP[1], SQ[0], SQ[1]], writes=[MIX[4 + j]])

        o_t = ppt[:, 0, :].rearrange("p (t e) -> p t e", t=4)
        t1_t = ppt[:, 1, :].rearrange("p (t e) -> p t e", t=4)
        on_t = ppt[:, 2, :].rearrange("p (t e) -> p t e", t=4)
        osq_t = t1_t
        pp_state = {"n": 0, "ob": 0}

        pending = []

        def postproc(ob, nt, h, tok0, n):
            O = Ocopy[:, ob]
            OB = OC[ob]
            pp_state["n"] += 1
            S.op("dve", lambda e: e.reciprocal(out=rs[:, :, 0:nt], in_=O[:, :, 0:nt, 128]), reads=[OB], writes=[SM])
            S.op("dve", lambda e: e.tensor_scalar(out=r1l[:, 0:nt], in0=rs[:, 1, 0:nt], scalar1=neglam, scalar2=None, op0=ALU.mult),
                 reads=[SM, LAMB], writes=[SM])
            S.op("dve", lambda e: e.tensor_tensor(out=o_t[:, 0:nt, :], in0=O[:, 0, 0:nt, 0:128],
                                                  in1=rs[:, 0, 0:nt].unsqueeze(2).to_broadcast([128, nt, 128]), op=ALU.mult),
                 reads=[OB, SM], writes=[PP0])
            S.op("dve", lambda e: e.tensor_tensor(out=t1_t[:, 0:nt, :], in0=O[:, 1, 0:nt, 0:128],
                                                  in1=r1l[:, 0:nt].unsqueeze(2).to_broadcast([128, nt, 128]), op=ALU.mult),
                 reads=[OB, SM], writes=[PP1])
            S.op("dve", lambda e: e.tensor_tensor(out=o_t[:, 0:nt, :], in0=o_t[:, 0:nt, :], in1=t1_t[:, 0:nt, :], op=ALU.add),
                 reads=[PP0, PP1], writes=[PP0])
            S.op("dve", lambda e: e.tensor_tensor(out=osq_t[:, 0:nt, :], in0=o_t[:, 0:nt, :], in1=o_t[:, 0:nt, :], op=ALU.mult),
                 reads=[PP0], writes=[PP1])
            S.op("dve", lambda e: e.tensor_reduce(out=ss4[:, 0:nt], in_=osq_t[:, 0:nt, :], axis=AX.X, op=ALU.add),
                 reads=[PP1], writes=[SS4])

            def stage2():
                S.op("act", lambda e: e.activation(out=ss4[:, 0:nt], in_=ss4[:, 0:nt], func=AF.Ln, bias=epsc, scale=1.0 / 128.0),
                     reads=[SS4, SM0], writes=[SS4])
                S.op("act", lambda e: e.activation(out=rstd4[:, 0:nt], in_=ss4[:, 0:nt], func=AF.Exp, scale=-0.5),
                     reads=[SS4], writes=[SS4])
                S.op("dve", lambda e: e.tensor_tensor(out=on_t[:, 0:nt, :], in0=o_t[:, 0:nt, :],
                                                      in1=rstd4[:, 0:nt].unsqueeze(2).to_broadcast([128, nt, 128]), op=ALU.mult),
                     reads=[PP0, SS4], writes=[PP2])
                S.op("dve", lambda e: e.tensor_tensor(out=on_t[:, 0:nt, :], in0=on_t[:, 0:nt, :], in1=gsub4[:, 0:nt, :], op=ALU.mult),
                     reads=[PP2, GSUB], writes=[PP2])

            def stage3():
                while conv_state["open"]:
                    next(cg_, None)
                tb = 6
                for t in range(nt):
                    S.op("pe", lambda e, t=t: e.transpose(out=bank(tb, 128, 128 * t), in_=on_t[:, t, :], identity=ident[:]),
                         reads=[PP2, IDENT], writes=[PB[tb]])
                S.op("dve", lambda e: e.tensor_copy(out=mixT[:, h, tok0:tok0 + 128 * nt], in_=bank(tb, 128 * nt)),
                     excl=[PB[tb]], writes=[MIX[h]])
            d2, d3 = (8, 12) if (nt == 4 and n + 13 < 192) else (3, 6)
            pending.append((n + d2, stage2))
            pending.append((n + d3, stage3))

        def run_pending(n):
            while pending and pending[0][0] <= n:
                pending.pop(0)[1]()

        SBANKS = [0, 1, 7]
        iters = []
        for h in range(4):
            for qb in range(2):
                for m in range(2):
                    for ci in range(12):
                        if ci < 8:
                            chunk = (128 * ci, ci, [36])
                        else:
                            chunk = (1280 + 128 * (ci - 8), 10 + ci - 8, [36])
                        iters.append((h, m, 512 * qb, 512, 4, ci, 12, chunk))
        for h in range(4):
            for m in range(2):
                for ci in range(2):
                    iters.append((h, m, 1024, 256, 2, ci, 2, (1024 + 128 * ci, 8 + ci, [36])))

        def emit_S(n):
            h, m, q0, nq, nt, ci, nch, (kc0, vt, bcols) = iters[n]
            sb_ = SBANKS[n % 3]
            Qm = QT if m == 0 else QT1
            S.op("pe", lambda e: e.matmul(bank(sb_, nq), lhsT=KT[:, h, kc0:kc0 + 128],
                                          rhs=Qm[:, h, q0:q0 + nq], start=True, stop=True),
                 reads=[KTB[h], KTC, QTB[h], QT1B], writes=[PB[sb_]])

        def emit_exp_pv(n):
            h, m, q0, nq, nt, ci, nch, (kc0, vt, bcols) = iters[n]
            sb_ = SBANKS[n % 3]
            pt_ = n % 3
            ob0 = 2 if m == 0 else 4
            seg = nq // len(bcols)
            for si, bc in enumerate(bcols):
                S.op("act", lambda e, si=si, bc=bc: e.activation(
                    out=PT[:, pt_, si * seg:(si + 1) * seg], in_=bank(sb_, seg, si * seg), func=AF.Exp,
                    bias=biasAll[:, bc:bc + 1], scale=0.125),
                    reads=[BIASB], excl=[PB[sb_]], writes=[PTH[pt_][si] if len(bcols) == 2 else PTB[pt_]])
            for t in range(nt):
                bb = t // 2
                if nt == 4 and vt < 8 and (q0 + 128 * t) // 256 != vt // 2:
                    vsrc, VB = Vo[:, vt, h, 0:129], VOB[vt]
                else:
                    vsrc, VB = Vaug[:, vt, h, 0:129], VA[vt]
                S.op("pe", lambda e, t=t, bb=bb, vsrc=vsrc: e.matmul(
                    bank(ob0 + bb, 129, 129 * (t % 2)), lhsT=PT[:, pt_, 128 * t:128 * t + 128], rhs=vsrc,
                    start=(ci == 0 and t % 2 == 0), stop=(ci == nch - 1), skip_group_check=True),
                    reads=[PTB[pt_], VB], writes=[PB[ob0 + bb]])
            if ci == nch - 1:
                ob = pp_state["ob"]
                nbank = (nt + 1) // 2
                for bb in range(nbank):
                    ntb = min(2, nt - 2 * bb)
                    S.op("dve", lambda e, bb=bb, ntb=ntb: e.tensor_copy(
                        out=Ocopy[:, ob, m, 2 * bb:2 * bb + ntb, :], in_=bank(ob0 + bb, 129 * ntb).rearrange("p (t e) -> p t e", t=ntb)),
                        excl=[PB[ob0 + bb]], writes=[OC[ob]])
                if m == 1:
                    postproc(ob, nt, h, q0, n)
                    pp_state["ob"] = 1 - ob

        prefetch(2)
        cg_ = conv_gen()
        for _ in cg_:
            pass
        S.alias(VOB, UT)
        S.op("dve", lambda e: e.tensor_scalar(out=sflag, in0=cflag, scalar1=-1.0, scalar2=1.0, op0=ALU.mult, op1=ALU.add),
             reads=[CST], writes=[SFL])
        for t in range(8):
            S.op("act", lambda e, t=t: e.activation(out=Vo[:, t, :, :].rearrange("p h e -> p (h e)"),
                                                     in_=Vaug[:, t, :, :].rearrange("p h e -> p (h e)"),
                                                     func=AF.Copy, scale=sflag),
                 reads=[VA[t], SFL], writes=[VOB[t]])
        for t in range(10, 14):
            S.op("act", lambda e, t=t: e.activation(out=Vaug[:, t, :, :].rearrange("p h e -> p (h e)"),
                                                     in_=Vaug[:, t, :, :].rearrange("p h e -> p (h e)"),
                                                     func=AF.Copy, scale=sflag),
                 reads=[VA[t], SFL], writes=[VA[t]])
        emit_S(0)
        emit_S(1)
        for n in range(len(iters)):
            if n + 2 < len(iters):
                emit_S(n + 2)
            emit_exp_pv(n)
            run_pending(n)
        run_pending(10 ** 9)
        for _ in cg_:
            pass

        S.alias(UT, VOB)
        S.alias(YCM, R2_MIX)
        wostate = {}

        def wofn(m):
            if m % 4 == 0:
                wostate["w"] = next_piece(8)
            w, wb = wostate["w"]
            return w, wb, 128 * (m % 4)

        evs = out_proj(1, 8, wofn, [mixT[:, k, :] for k in range(8)], MIX, ycM, YCM,
                       mod_next=lambda: (modulation(2, "a", defer=True), modulation(2, "b", defer=True)))
        S.alias(HT, YCM + R2_MIX)
        evs[0]()
        prenorm_apply(2)
        evs[1]()
        ffn(2, last=True)

        for c in range(8):
            S.op("sp", lambda e, c=c: e.dma_start(out=yT_o[c], in_=xT[:, c, :]), reads=[XT[c]], dma=True)

        S.emit(nc, st)
    return nc


def _rope_tables():
    t = np.arange(1024)
    row = (t // 64).astype(np.float32)
    col = (t % 64).astype(np.float32)
    half = 32
    freqs = (10000.0 ** (-np.arange(0, half, 2, dtype=np.float32) / half)).astype(np.float32)
    C = np.zeros((128, 1024), np.float32)
    Sg = np.zeros((128, 1024), np.float32)
    P = np.zeros((128, 128), np.float32)
    for p in range(128):
        d = p % 64
        pos = row if d < 32 else col
        dd = d % 32
        f = freqs[dd % 16]
        ang = (pos * f).astype(np.float32)
        C[p] = np.cos(ang)
        if dd < 16:
            Sg[p] = -np.sin(ang)
            partner = p + 16
        else:
            Sg[p] = np.sin(ang)
            partner = p - 16
        P[partner, p] = 1.0
    return C, Sg, P


def _pieces_cols(W, col_lists):
    K = W.shape[0] // 128
    out = []
    for cols in col_lists:
        sub = W[:, cols]
        sub = sub.reshape(K, 128, len(cols)).transpose(1, 0, 2)
        out.append(np.ascontiguousarray(sub).reshape(128, K * len(cols)))
    return np.stack(out, 0)


_PROG = {}


def kernel(x_prompt, x_sample, c, cache_k, cache_v, c_ctx, w_mod, b_mod, norm_pre, norm_post,
           ffn1_up, ffn1_down, ffn2_up, ffn2_down, w_in, conv_w, lam_qk, subln_g, w_o):
    f = lambda a: np.ascontiguousarray(np.asarray(a, dtype=np.float32))
    x_prompt, x_sample, c, cache_k, cache_v, c_ctx = map(f, (x_prompt, x_sample, c, cache_k, cache_v, c_ctx))
    w_mod, b_mod, norm_pre, norm_post = f(w_mod)[0], f(b_mod)[0], f(norm_pre)[0], f(norm_post)[0]
    ffn1_up, ffn1_down, ffn2_up, ffn2_down = f(ffn1_up)[0], f(ffn1_down)[0], f(ffn2_up)[0], f(ffn2_down)[0]
    w_in, conv_w, lam_qk, subln_g, w_o = f(w_in)[0], f(conv_w)[0], f(lam_qk)[0], f(subln_g)[0], f(w_o)[0]

    ar = np.arange
    wmodP = _pieces_cols(w_mod, [ar(512 * i, 512 * i + 512) for i in range(18)])
    upcols = [np.concatenate([ar(256 * j, 256 * j + 256), ar(2816 + 256 * j, 2816 + 256 * j + 256)]) for j in range(11)]
    up1P = _pieces_cols(ffn1_up, upcols)
    up2P = _pieces_cols(ffn2_up, upcols)
    dn1P = _pieces_cols(ffn1_down, [ar(256 * j, 256 * j + 256) for j in range(4)])
    dn2P = _pieces_cols(ffn2_down, [ar(256 * j, 256 * j + 256) for j in range(4)])
    wqP = _pieces_cols(w_in, [ar(0, 512)])[0]
    wkP = _pieces_cols(w_in, [ar(512, 1024)])[0]
    wvP = _pieces_cols(w_in, [ar(1024, 1536)])[0]
    wcP = _pieces_cols(w_in, [np.concatenate([ar(2560 + 128 * j, 2560 + 128 * j + 128), ar(2048 + 128 * j, 2048 + 128 * j + 128),
                                              ar(1536 + 128 * j, 1536 + 128 * j + 128)]) for j in range(4)])
    woP = _pieces_cols(w_o, [ar(0, 512), ar(512, 1024)])

    ropeC, ropeS, perm = _rope_tables()
    ident = np.eye(128, dtype=np.float32)
    gsub = np.ascontiguousarray(np.broadcast_to(np.tile(subln_g, 4)[None, :], (128, 512))).astype(np.float32)

    def tmaj(v):
        return np.ascontiguousarray(v.reshape(-1, 128).T)

    shared_cst = np.zeros((128, NCST), np.float32)
    bm = tmaj(b_mod)
    shared_cst[:, C_BMOD:C_BMOD + 144] = np.repeat(bm, 2, axis=1)
    gp = np.stack([tmaj(norm_pre[s]) for s in range(3)], 1)
    shared_cst[:, C_GPRE:C_GPRE + 48] = np.repeat(gp.reshape(128, 24), 2, axis=1)
    gq = np.stack([tmaj(norm_post[s]) for s in range(3)], 1)
    shared_cst[:, C_GPOST:C_GPOST + 48] = np.repeat(gq.reshape(128, 24), 2, axis=1)
    cw = np.stack([tmaj(conv_w[i]) for i in range(3)], 2)
    shared_cst[:, C_CONVW:C_CONVW + 12] = cw.reshape(128, 12)
    shared_cst[:, C_LAM:C_LAM + 256] = lam_qk.reshape(1, 256)

    in_maps = []
    groups = []
    for core in range(8):
        if core < 6:
            tokA = x_prompt[5 * core:5 * core + 4].reshape(1024, 1024)
            tokB = x_prompt[5 * core + 4]
            condA = c_ctx
            sample = None
        else:
            sample = core - 6
            tokA = x_sample[sample]
            tokB = x_prompt[30 + sample]
            condA = c[sample]
        tok = np.concatenate([tokA, tokB], 0)
        xT = np.ascontiguousarray(tok.T).reshape(8, 128, NT)
        cst = shared_cst.copy()
        cond2 = np.stack([tmaj(condA), tmaj(c_ctx)], 2)
        cst[:, C_COND:C_COND + 16] = cond2.reshape(128, 16)
        mask = np.zeros(37, np.float32)
        if sample is None:
            for cc in range(8):
                for j in range(4):
                    mask[cc * 4 + j] = 0.0 if (cc // 2) == j else NEG
            mask[32:36] = NEG
            cst[:, C_FLAG] = 1.0
            rc, rs_ = np.ones_like(ropeC), np.zeros_like(ropeS)
            cb = 0
        else:
            rc, rs_ = ropeC, ropeS
            cb = sample
        cst[:, C_MASK:C_MASK + 37] = mask[None, :]
        ck = cache_k[cb, 0]
        kcT = np.ascontiguousarray(ck.transpose(2, 1, 0))
        cv = cache_v[cb, 0].reshape(4, 128, 512)
        vc = np.ascontiguousarray(cv.transpose(1, 0, 2))
        in_maps.append({
            "xT": xT, "cst": cst, "gsub": gsub, "ident": ident, "perm": perm, "ropeC": rc, "ropeS": rs_,
            "kcT": kcT, "vc": vc, "wmod": wmodP, "up1": up1P, "up2": up2P, "dn1": dn1P, "dn2": dn2P,
            "wq": wqP, "wk": wkP, "wv": wvP, "wc": wcP, "wo": woP,
        })

    if "nc" not in _PROG:
        _PROG["nc"] = build_program()
    res = run_bass_kernel_spmd(_PROG["nc"], in_maps, core_ids=list(range(8)))

    y_prompt = np.zeros((32, 256, 1024), np.float32)
    y_sample = np.zeros((2, 1024, 1024), np.float32)
    new_k = np.zeros((32, 1, 256, 4, 128), np.float32)
    new_v = np.zeros((32, 1, 256, 4, 128), np.float32)
    for core in range(8):
        r = res.results[core]
        y = r["yT"].reshape(1024, NT).T
        k = r["kT"].reshape(4, 128, NT).transpose(2, 0, 1)
        v = r["vout"].reshape(NT, 4, 128)
        if core < 6:
            for i in range(4):
                y_prompt[5 * core + i] = y[256 * i:256 * i + 256]
                new_k[5 * core + i, 0] = k[256 * i:256 * i + 256]
                new_v[5 * core + i, 0] = v[256 * i:256 * i + 256]
            bB = 5 * core + 4
        else:
            y_sample[core - 6] = y[0:1024]
            bB = 30 + core - 6
        y_prompt[bB] = y[1024:1280]
        new_k[bB, 0] = k[1024:1280]
        new_v[bB, 0] = v[1024:1280]
    return (y_prompt, y_sample, new_k, new_v)
```
